# Optimizing a Trainium2 kernel written in Bass

```python
import jax, jax.numpy as jnp
from jax import lax
import numpy as np

D_MODEL = 4096
BATCH = 4
SEQ = 4096
DEPTH = 1

HEAD_DIM = 128
N_HEADS = (D_MODEL // 2) // HEAD_DIM
N_KV_HEADS = N_HEADS // 4
GQA_GROUP = N_HEADS // N_KV_HEADS
WINDOW = 128
BLOCK = 128
CONV_WIDTH = D_MODEL // 2
CONV_K = 3
Q_W = N_HEADS * HEAD_DIM
KV_W = N_KV_HEADS * HEAD_DIM
GATE_W = D_MODEL
OFF_Q = 0
OFF_K = OFF_Q + Q_W
OFF_V = OFF_K + KV_W
OFF_H = OFF_V + KV_W
OFF_B = OFF_H + CONV_WIDTH
OFF_C = OFF_B + CONV_WIDTH
OFF_GA = OFF_C + CONV_WIDTH
OFF_GC = OFF_GA + GATE_W
IN_W = OFF_GC + GATE_W
PEER_HEADS = 8
N_KEYS = 128
N_EXPERTS = N_KEYS * N_KEYS
PEER_KEY_DIM = 128
PEER_TOPK = 16
PEER_CHUNK = 128
EPS = 1e-6

kernel_name = "hybrid_swa_shortconv_peer_encoder"


def rms_norm(x, g):
    xf = x.astype(jnp.float32)
    y = xf * lax.rsqrt(jnp.mean(xf * xf, axis=-1, keepdims=True) + EPS)
    return (y * g.astype(jnp.float32)).astype(x.dtype)


def alibi_slopes():
    h = jnp.arange(1, N_HEADS + 1, dtype=jnp.float32)
    return jnp.exp2(-8.0 * h / N_HEADS).reshape(N_KV_HEADS, GQA_GROUP)


def banded_window_attention(q, k, v, sink_logits):
    b, s = q.shape[0], q.shape[1]
    nb = s // BLOCK
    qb = q.reshape(b, nb, BLOCK, N_KV_HEADS, GQA_GROUP, HEAD_DIM)

    def neighbour_blocks(t):
        tp = jnp.pad(t, ((0, 0), (BLOCK, BLOCK), (0, 0), (0, 0)))
        tb = tp.reshape(b, nb + 2, BLOCK, N_KV_HEADS, HEAD_DIM)
        return jnp.concatenate([tb[:, :-2], tb[:, 1:-1], tb[:, 2:]], axis=2)

    kb = neighbour_blocks(k)
    vb = neighbour_blocks(v)
    scores = jnp.einsum('bnqhgd,bnshd->bnhgqs', qb, kb,
                        preferred_element_type=jnp.float32) * (HEAD_DIM ** -0.5)
    qi = jnp.arange(BLOCK)[:, None]
    kj = jnp.arange(3 * BLOCK)[None, :]
    dist = jnp.abs(kj - BLOCK - qi)
    key_pos = (jnp.arange(nb)[:, None] - 1) * BLOCK + jnp.arange(3 * BLOCK)[None, :]
    in_range = (key_pos >= 0) & (key_pos < s)
    mask = (dist <= WINDOW)[None, :, :] & in_range[:, None, :]
    slopes = alibi_slopes()
    scores = scores - slopes[:, :, None, None] * dist.astype(jnp.float32)
    scores = jnp.where(mask[None, :, None, None], scores, -jnp.inf)
    sink = sink_logits.astype(jnp.float32).reshape(N_KV_HEADS, GQA_GROUP)[:, :, None, None]
    m = jnp.maximum(jnp.max(scores, axis=-1, keepdims=True), sink)
    p = jnp.exp(scores - m)
    denom = jnp.sum(p, axis=-1, keepdims=True) + jnp.exp(sink - m)
    probs = (p / denom).astype(v.dtype)
    out = jnp.einsum('bnhgqs,bnshd->bnqhgd', probs, vb)
    return out.reshape(b, s, Q_W)


def short_gated_conv(h, gate_b, gate_c, conv_w):
    u = gate_c * h
    half = CONV_K // 2
    s = u.shape[1]
    up = jnp.pad(u, ((0, 0), (half, half), (0, 0)))
    y = sum(conv_w[j] * up[:, j:j + s] for j in range(CONV_K))
    return gate_b * y


def peer_ffn(x, w_q_peer, sub_keys, w_down, w_up):
    b, s, d = x.shape
    t = b * s
    xt = x.reshape(t, d)
    q = (xt @ w_q_peer).reshape(t, PEER_HEADS, 2, PEER_KEY_DIM)
    s1 = jnp.einsum('thk,hnk->thn', q[:, :, 0], sub_keys[:, 0], preferred_element_type=jnp.float32)
    s2 = jnp.einsum('thk,hnk->thn', q[:, :, 1], sub_keys[:, 1], preferred_element_type=jnp.float32)
    v1, i1 = lax.top_k(s1, PEER_TOPK)
    v2, i2 = lax.top_k(s2, PEER_TOPK)
    cand = (v1[..., :, None] + v2[..., None, :]).reshape(t, PEER_HEADS, PEER_TOPK * PEER_TOPK)
    cand_idx = (i1[..., :, None] * N_KEYS + i2[..., None, :]).reshape(t, PEER_HEADS, PEER_TOPK * PEER_TOPK)
    top_s, pos = lax.top_k(cand, PEER_TOPK)
    idx = jnp.take_along_axis(cand_idx, pos, axis=-1)
    gates = jax.nn.softmax(top_s, axis=-1)
    n_chunks = t // PEER_CHUNK

    def chunk(args):
        xc, ic, gc = args
        u = jnp.take(w_down, ic, axis=0)
        a = jax.nn.gelu(jnp.einsum('cd,chkd->chk', xc, u, preferred_element_type=jnp.float32),
                        approximate=False)
        wgt = (gc * a).astype(xc.dtype)
        vv = jnp.take(w_up, ic, axis=0)
        return jnp.einsum('chk,chkd->cd', wgt, vv)

    out = lax.map(chunk, (xt.reshape(n_chunks, PEER_CHUNK, d),
                          idx.reshape(n_chunks, PEER_CHUNK, PEER_HEADS, PEER_TOPK),
                          gates.reshape(n_chunks, PEER_CHUNK, PEER_HEADS, PEER_TOPK)))
    return out.reshape(b, s, d)


def setup_inputs(seed: int = 0) -> dict:
    key = jax.random.key(seed)
    ks = jax.random.split(key, 16)
    f32 = jnp.float32
    nrm = lambda k, shape, scale: jax.random.normal(k, shape, f32) * scale
    L = DEPTH
    return {
        "x": nrm(ks[0], (BATCH, SEQ, D_MODEL), 1.0),
        "norm1_g": 1.0 + nrm(ks[1], (L, D_MODEL), 0.01),
        "w_in": nrm(ks[2], (L, D_MODEL, IN_W), D_MODEL ** -0.5),
        "q_norm_g": 1.0 + nrm(ks[3], (L, HEAD_DIM), 0.01),
        "k_norm_g": 1.0 + nrm(ks[4], (L, HEAD_DIM), 0.01),
        "sink_logits": nrm(ks[5], (L, N_HEADS), 0.5),
        "conv_w": nrm(ks[6], (L, CONV_K, CONV_WIDTH), CONV_K ** -0.5),
        "w_o_attn": nrm(ks[7], (L, Q_W, D_MODEL), Q_W ** -0.5),
        "w_o_conv": nrm(ks[8], (L, CONV_WIDTH, D_MODEL), CONV_WIDTH ** -0.5),
        "w_out": nrm(ks[9], (L, D_MODEL, D_MODEL), D_MODEL ** -0.5),
        "norm2_g": 1.0 + nrm(ks[10], (L, D_MODEL), 0.01),
        "w_q_peer": nrm(ks[11], (L, D_MODEL, PEER_HEADS * 2 * PEER_KEY_DIM), D_MODEL ** -0.5),
        "sub_keys": nrm(ks[12], (L, PEER_HEADS, 2, N_KEYS, PEER_KEY_DIM), PEER_KEY_DIM ** -0.5),
        "w_down": nrm(ks[13], (L, N_EXPERTS, D_MODEL), D_MODEL ** -0.5),
        "w_up": nrm(ks[14], (L, N_EXPERTS, D_MODEL), PEER_HEADS ** -0.5 * PEER_TOPK ** -0.5),
    }


def reference(x, norm1_g, w_in, q_norm_g, k_norm_g, sink_logits, conv_w, w_o_attn, w_o_conv,
              w_out, norm2_g, w_q_peer, sub_keys, w_down, w_up):
    b, s, _ = x.shape
    for i in range(DEPTH):
        xn = rms_norm(x, norm1_g[i])
        proj = xn @ w_in[i]
        q = rms_norm(proj[..., OFF_Q:OFF_K].reshape(b, s, N_HEADS, HEAD_DIM), q_norm_g[i])
        k = rms_norm(proj[..., OFF_K:OFF_V].reshape(b, s, N_KV_HEADS, HEAD_DIM), k_norm_g[i])
        v = proj[..., OFF_V:OFF_H].reshape(b, s, N_KV_HEADS, HEAD_DIM)
        attn = banded_window_attention(q, k, v, sink_logits[i])
        conv = short_gated_conv(proj[..., OFF_H:OFF_B], proj[..., OFF_B:OFF_C],
                                proj[..., OFF_C:OFF_GA], conv_w[i])
        branch_a = attn @ w_o_attn[i]
        branch_c = conv @ w_o_conv[i]
        merged = (jax.nn.sigmoid(proj[..., OFF_GA:OFF_GC]) * branch_a
                  + jax.nn.sigmoid(proj[..., OFF_GC:IN_W]) * branch_c)
        x = x + merged @ w_out[i]
        x = x + peer_ffn(rms_norm(x, norm2_g[i]), w_q_peer[i], sub_keys[i], w_down[i], w_up[i])
    return x
```

```python
import contextlib
import math
import os
import numpy as np
import concourse.bass as bass
import concourse.mybir as mybir
from concourse.bass_utils import run_bass_kernel_spmd

F32 = mybir.dt.float32
BF16 = mybir.dt.bfloat16
U32 = mybir.dt.uint32
AF = mybir.ActivationFunctionType
ALU = mybir.AluOpType
AX = mybir.AxisListType

EPS = 1e-6
NCORES = 8
ENGS = ("pe", "act", "dve", "pool", "sp")


class Op:
    __slots__ = ("eng", "fn", "deps", "is_dma", "semkey", "tick", "marked", "idx")

    def __init__(self, eng, fn, is_dma=False, semkey=None):
        self.eng = eng
        self.fn = fn
        self.deps = []
        self.is_dma = is_dma
        self.semkey = semkey
        self.tick = None
        self.marked = False


class Prog:
    def __init__(self, nc):
        self.nc = nc
        self.ops = []
        self.res = {}
        self.dma_counts = {}
        self.last = {}
        self.open_dmas = []

    def _add(self, op, reads, writes):
        op.idx = len(self.ops)
        deps = set()
        for r in reads:
            st = self.res.get(r)
            if st is not None:
                deps.update(st[0])
        for w in writes:
            st = self.res.get(w)
            if st is not None:
                deps.update(st[0])
                deps.update(st[1])
        for r in reads:
            st = self.res.setdefault(r, [[], []])
            st[1].append(op.idx)
        for w in writes:
            self.res[w] = [[op.idx], []]
        deps.discard(op.idx)
        op.deps = deps
        self.ops.append(op)
        if op.is_dma:
            self.open_dmas.append(op.idx)
        elif op.fn is not None:
            self.last[op.eng] = op.idx
        return op

    def op(self, eng, fn, reads=(), writes=()):
        return self._add(Op(eng, fn), reads, writes)

    def dma(self, eng, fn, semkey, reads=(), writes=()):
        op = Op(eng, fn, is_dma=True, semkey=semkey)
        self.dma_counts[semkey] = self.dma_counts.get(semkey, 0) + 1
        op.tick = 16 * self.dma_counts[semkey]
        return self._add(op, reads, writes)

    def barrier(self):
        deps = set(self.last.values()) | set(self.open_dmas)
        self.open_dmas = []
        for e in ENGS:
            o = Op(e, None)
            o.idx = len(self.ops)
            o.deps = set(deps)
            self.ops.append(o)
        self.res = {}

    def emit(self):
        nc = self.nc
        ops = self.ops
        for o in ops:
            latest = {}
            keep = []
            for d in o.deps:
                p = ops[d]
                if p.is_dma:
                    keep.append(d)
                else:
                    if p.eng == o.eng and not o.is_dma and p.eng == "pe":
                        continue
                    if p.eng not in latest or latest[p.eng] < d:
                        latest[p.eng] = d
            keep.extend(latest.values())
            o.deps = sorted(keep)
            for d in o.deps:
                ops[d].marked = True
        cnt = {e: 0 for e in ENGS}
        for o in ops:
            if o.is_dma or o.fn is None:
                continue
            if o.marked:
                cnt[o.eng] += 1
                o.tick = cnt[o.eng]
        self.stats = dict(cnt)
        streams = {e: [o for o in ops if o.eng == e] for e in ENGS}
        stack = contextlib.ExitStack()
        with stack:
            esem = {e: stack.enter_context(nc.semaphore("S_" + e)) for e in ENGS}
            dsem = {}
            for n, k in enumerate(self.dma_counts):
                dsem[k] = stack.enter_context(nc.semaphore("D%d" % n))
            block = stack.enter_context(nc.Block())

            def run_stream(e, engobj):
                waited = {}
                for o in streams[e]:
                    for d in o.deps:
                        p = ops[d]
                        if p.is_dma:
                            key = ("d", p.semkey)
                            sem = dsem[p.semkey]
                        else:
                            key = ("e", p.eng)
                            sem = esem[p.eng]
                        if waited.get(key, 0) >= p.tick:
                            continue
                        engobj.wait_ge(sem, p.tick)
                        waited[key] = p.tick
                    if o.fn is None:
                        continue
                    ins = o.fn(engobj)
                    if o.is_dma:
                        ins.then_inc(dsem[o.semkey], 16)
                    elif o.marked:
                        ins.then_inc(esem[o.eng], 1)
                last = {}
                for o in streams[e]:
                    if o.is_dma:
                        last[o.semkey] = max(last.get(o.semkey, 0), o.tick)
                for k, v in last.items():
                    if waited.get(("d", k), 0) < v:
                        engobj.wait_ge(dsem[k], v)

            @block.tensor
            def _(eng):
                run_stream("pe", eng)

            @block.scalar
            def _(eng):
                run_stream("act", eng)

            @block.vector
            def _(eng):
                run_stream("dve", eng)

            @block.gpsimd
            def _(eng):
                run_stream("pool", eng)

            @block.sync
            def _(eng):
                run_stream("sp", eng)


NB_W = 3
OFF_A = 0
OFF_B = 65536
OFF_C = 98304
OFF_D = 114688
OFF_E = 139264
E_SIZE = 57344
ARENA = OFF_E + E_SIZE


def build_program(NT=4, debug=False, stop_after=None):
    nc = bass.Bass("TRN2", target_bir_lowering=False)
    dt_in = lambda name, shape, dt=F32: nc.dram_tensor(name, list(shape), dt, kind="ExternalInput").ap()
    xh = dt_in("xh", [2304, 4096])
    emask = dt_in("emask", [128, 2])
    w_in_r = dt_in("w_in_r", [136, 128, 32, 128])
    w_oaoc_r = dt_in("w_oaoc_r", [32, 128, 32, 128])
    w_out_r = dt_in("w_out_r", [32, 128, 32, 128])
    w_qp_r = dt_in("w_qp_r", [16, 128, 32, 128])
    w_dn_r = dt_in("w_dn_r", [128, 128, 32, 128])
    w_up_r = dt_in("w_up_r", [8, 16, 128, 32, 128])
    g1_d = dt_in("g1", [128, 32])
    g2_d = dt_in("g2", [128, 32])
    qkg_d = dt_in("qkg", [128, 2])
    sinkb_d = dt_in("sinkb", [128, 16])
    convw_d = dt_in("convw", [128, 3, 16])
    subk_d = dt_in("subk", [128, 16, 128])
    ident_d = dt_in("ident", [128, 128])
    nd_d = dt_in("nd", [128, 3, 128])
    iota_d = dt_in("iota", [128, 128])
    out_d = nc.dram_tensor("out", [2048, 4096], F32, kind="ExternalOutput").ap()
    gscr = nc.dram_tensor("gscr", [NT, 16, 128, 128, 32], BF16).ap()
    hscr = nc.dram_tensor("hscr", [NT, 2, 128, 16, 512], BF16).ap()
    dbg_tensors = {}

    wdram = {}
    for c in range(136):
        wdram[("in", c)] = (w_in_r[c], 32)
    for c in range(32):
        wdram[("oaoc", c)] = (w_oaoc_r[c], 32)
        wdram[("out", c)] = (w_out_r[c], 32)
    for c in range(16):
        wdram[("qp", c)] = (w_qp_r[c], 32)
    for i in range(128):
        wdram[("dn", i)] = (w_dn_r[i], 32)
    for s in range(8):
        for c in range(16):
            wdram[("up", s, c)] = (w_up_r[s, c], 32)

    st = contextlib.ExitStack()
    with st:
        sbt = lambda name, shape, dt: st.enter_context(nc.sbuf_tensor(name, list(shape), dt))
        arena = sbt("arena", [128, ARENA // 2], BF16)
        ps = st.enter_context(nc.psum_tensor("ps", [128, 8, 512], F32))
        identf = sbt("identf", [128, 128], F32)
        onesf = sbt("onesf", [128, 128], F32)
        onesb = sbt("onesb", [128, 128], BF16)
        nd = sbt("ndt", [128, 3, 128], F32)
        iota = sbt("iotat", [128, 128], F32)
        g1 = sbt("g1t", [128, 32], F32)
        g2 = sbt("g2t", [128, 32], F32)
        qkg = sbt("qkgt", [128, 2], F32)
        kgs = sbt("kgs", [128, 1], F32)
        sinkb = sbt("sinkbt", [128, 16], F32)
        esink = sbt("esink", [128, 16], F32)
        convw = sbt("convwt", [128, 3, 16], F32)
        subk = sbt("subkt", [128, 16, 128], F32)
        emk = sbt("emk", [128, 2], F32)
        epsc = sbt("epsc", [128, 2], F32)

        def cv(off, shape, dt):
            n = 1
            for s_ in shape:
                n *= s_
            nbytes = n * (2 if dt == BF16 else 4)
            assert off % 4 == 0
            a = arena[:, off // 2:(off + nbytes) // 2]
            if dt != BF16:
                a = a.bitcast(dt)
            if len(shape) == 2:
                a = a.rearrange("p (a b) -> p a b", a=shape[0])
            elif len(shape) == 3:
                a = a.rearrange("p (a b c) -> p a b c", a=shape[0], b=shape[1])
            elif len(shape) == 4:
                a = a.rearrange("p (a b c d) -> p a b c d", a=shape[0], b=shape[1], c=shape[2])
            return a

        E = OFF_E
        xnT = cv(OFF_A, [32, 768], BF16)
        attnT = cv(OFF_A + 49152, [16, 512], BF16)
        acc = cv(OFF_A, [32, 512], F32)
        mT = cv(OFF_B, [32, 512], BF16)
        xn2T = cv(OFF_B, [32, 512], BF16)
        convT = cv(OFF_C, [16, 512], BF16)
        wbuf = [cv(OFF_D + 8192 * i, [32, 128], BF16) for i in range(NB_W)]

        def body(P, W):
            def dbg(name, src_ap, shape, dt, reads=()):
                if not debug:
                    return
                if name not in dbg_tensors:
                    dbg_tensors[name] = nc.dram_tensor("dbg_" + name, [128] + list(shape), dt, kind="ExternalOutput").ap()
                d = dbg_tensors[name]
                P.barrier()
                P.dma("sp", lambda e: e.dma_start(out=d, in_=src_ap), ("dbg", name))
                P.barrier()

            def mm(out, lhsT, rhs, start, stop, reads, writes):
                P.op("pe", lambda e: e.matmul(out=out, lhsT=lhsT, rhs=rhs, start=start, stop=stop), reads, writes)

            for i, (t, d) in enumerate(((identf, ident_d), (nd, nd_d), (iota, iota_d), (g1, g1_d), (g2, g2_d), (qkg, qkg_d),
                                        (sinkb, sinkb_d), (convw, convw_d), (emk, emask))):
                P.dma("sp", lambda e, t=t, d=d: e.dma_start(out=t[:], in_=d), ("c", i), writes=[("const", i)])
            P.dma("sp", lambda e: e.dma_start(out=subk[:], in_=subk_d), ("c", 20), writes=[("const", 20)])
            P.op("dve", lambda e: e.memset(onesf[:], 1.0), writes=[("const", 30)])
            P.op("dve", lambda e: e.memset(onesb[:], 1.0), writes=[("const", 31)])
            P.op("dve", lambda e: e.memset(epsc[:, 0:1], EPS), writes=[("const", 34)])
            P.op("dve", lambda e: e.memset(epsc[:, 1:2], 128.0 * EPS), writes=[("const", 35)])
            P.op("dve", lambda e: e.tensor_scalar(out=kgs[:], in0=qkg[:, 1:2], scalar1=math.sqrt(128.0), scalar2=None, op0=ALU.mult),
                 reads=[("const", 5)], writes=[("const", 32)])
            P.op("act", lambda e: e.activation(out=esink[:], in_=sinkb[:], func=AF.Exp), reads=[("const", 6)], writes=[("const", 33)])
            P.barrier()

            def qknorm(src, n, gain, out, k, reads, writes, sbank=6):
                sq = cv(E + 16384 + 2048 * (k % 2), [512], F32)
                rs = cv(E + 20480 + 2048 * (k % 2), [512], F32)
                P.op("act", lambda e: e.activation(out=sq[:, 0:n], in_=src, func=AF.Square), reads=reads, writes=[("sqk", k % 2)])
                mm(ps[:, sbank, 0:n], onesf[:], sq[:, 0:n], True, True, [("sqk", k % 2)], [("ps", sbank)])
                P.op("act", lambda e: e.activation(out=rs[:, 0:n], in_=ps[:, sbank, 0:n], func=AF.Ln, bias=epsc[:, 1:2], scale=1.0),
                     reads=[("ps", sbank)], writes=[("rsk", k % 2)])
                P.op("act", lambda e: e.activation(out=rs[:, 0:n], in_=rs[:, 0:n], func=AF.Exp, scale=-0.5), reads=[("rsk", k % 2)], writes=[("rsk", k % 2)])
                P.op("dve", lambda e: e.scalar_tensor_tensor(out=out, in0=src, scalar=gain, in1=rs[:, 0:n], op0=ALU.mult, op1=ALU.mult),
                     reads=list(reads) + [("rsk", k % 2)], writes=writes)

            for tt in range(NT):
                r_tile = tt * 512
                xs_bufs = [cv(OFF_C, [4096], F32), cv(E, [4096], F32)]
                xTts = [cv(E + 16384, [32, 128], F32), cv(E + 37376, [32, 128], F32)]
                sq_bufs = [cv(E + 32768, [512], F32), cv(E + 34816, [512], F32)]
                rstd = cv(E + 36864, [128], F32)
                for wb in range(6):
                    xs = xs_bufs[wb % 2]
                    xTt = xTts[wb % 2]
                    xw2 = wb % 2
                    xsn = ("xs", wb % 2)
                    r0 = r_tile + wb * 128
                    P.dma("sp", lambda e, xs=xs, r0=r0: e.dma_start(out=xs, in_=xh[r0:r0 + 128, :]), xsn, writes=[xsn])
                    for g8 in range(8):
                        bk = g8 % 4
                        for q in range(4):
                            c = g8 * 4 + q
                            P.op("pe", lambda e, xs=xs, c=c, bk=bk, q=q: e.transpose(out=ps[:, bk, q * 128:(q + 1) * 128], in_=xs[:, c * 128:(c + 1) * 128], identity=identf[:]),
                                 reads=[xsn], writes=[("ps", bk)])
                        sq = sq_bufs[g8 % 2]
                        P.op("act", lambda e, sq=sq, bk=bk: e.activation(out=sq, in_=ps[:, bk, :], func=AF.Square), reads=[("ps", bk)], writes=[("sq", g8 % 2)])
                        for q in range(4):
                            c = g8 * 4 + q
                            P.op("act", lambda e, c=c, bk=bk, q=q, xTt=xTt: e.mul(out=xTt[:, c, :], in_=ps[:, bk, q * 128:(q + 1) * 128], mul=g1[:, c:c + 1]),
                                 reads=[("ps", bk)], writes=[("xTt", xw2, c)])
                        for q in range(4):
                            c = g8 * 4 + q
                            mm(ps[:, 7, 0:128], onesf[:], sq[:, q * 128:(q + 1) * 128], c == 0, c == 31, [("sq", g8 % 2)], [("ps", 7)])
                    P.op("act", lambda e: e.activation(out=rstd, in_=ps[:, 7, 0:128], func=AF.Sqrt, bias=epsc[:, 0:1], scale=1.0 / 4096),
                         reads=[("ps", 7)], writes=["rstd"])
                    P.op("dve", lambda e: e.reciprocal(out=rstd, in_=rstd), reads=["rstd"], writes=["rstd"])
                    P.op("dve", lambda e, wb=wb, xTt=xTt: e.tensor_tensor(out=xnT[:, :, wb * 128:(wb + 1) * 128], in0=xTt, in1=rstd.unsqueeze(1).broadcast_to([128, 32, 128]), op=ALU.mult),
                         reads=[("xTt", xw2, c) for c in range(32)] + ["rstd"], writes=[("xnT", wb)] + [("acc", c) for c in range(32)])
                xn_all = [("xnT", wb) for wb in range(6)]
                xn_main = [("xnT", wb) for wb in range(1, 5)]
                if tt == 0:
                    dbg("xnT", xnT, [32, 768], BF16)
                if stop_after == "S0":
                    continue

                KT = cv(E, [4, 768], BF16)
                V = cv(E + 6144, [6, 512], BF16)
                QTs = [cv(E + 12288, [4, 512], BF16), cv(E + 47104, [4, 512], BF16)]
                S2b = [cv(E + 24576 + 6144 * i, [3, 4, 128], F32) for i in range(2)]
                PTb = [cv(E + 36864 + 3072 * i, [3, 4, 128], BF16) for i in range(2)]
                denb = [cv(E + 43008 + 2048 * i, [4, 128], F32) for i in range(2)]
                kcnt = 0
                for hk in range(4):
                    wt, wn = W.get(P, ("in", 16 + hk))
                    for half, (w0, n) in enumerate(((0, 512), (512, 256))):
                        bk = half
                        for kc in range(32):
                            mm(ps[:, bk, 0:n], wt[:, kc, :], xnT[:, kc, w0:w0 + n], kc == 0, kc == 31, [wn] + xn_all, [("ps", bk)])
                        qknorm(ps[:, bk, 0:n], n, kgs[:, 0:1], KT[:, hk, w0:w0 + n], kcnt, [("ps", bk)], [("KT", hk, half)])
                        kcnt += 1
                for hk in range(4):
                    wt, wn = W.get(P, ("in", 20 + hk))
                    for wb in range(6):
                        bk = 2 + (wb % 2)
                        for kc in range(32):
                            mm(ps[:, bk, 0:128], xnT[:, kc, wb * 128:(wb + 1) * 128], wt[:, kc, :], kc == 0, kc == 31, [wn, ("xnT", wb)], [("ps", bk)])
                        P.op("act", lambda e, wb=wb, hk=hk, bk=bk: e.copy(out=V[:, wb, hk * 128:(hk + 1) * 128], in_=ps[:, bk, 0:128]),
                             reads=[("ps", bk)], writes=[("V", wb, hk)])
                if tt == 0:
                    dbg("KT", KT, [4, 768], BF16)
                    dbg("V", V, [6, 512], BF16)
                def q_chunk(hk, g, kq):
                    wt, wn = W.get(P, ("in", hk * 4 + g))
                    bk = g % 2
                    for kc in range(32):
                        mm(ps[:, bk, :], wt[:, kc, :], xnT[:, kc, 128:640], kc == 0, kc == 31, [wn] + xn_main, [("ps", bk)])
                    qknorm(ps[:, bk, :], 512, qkg[:, 0:1], QTs[hk % 2][:, g, :], kq, [("ps", bk)], [("QT", hk % 2, g)], sbank=2)

                for g in range(4):
                    q_chunk(0, g, kcnt)
                    kcnt += 1
                for hk in range(4):
                    QT = QTs[hk % 2]

                    def a_scores(n, hk=hk, QT=QT):
                        base = 3
                        for j in range(3):
                            kb = n + j
                            mm(ps[:, base + j, :], KT[:, hk, kb * 128:(kb + 1) * 128], QT[:, :, n * 128:(n + 1) * 128], True, True,
                               [("KT", hk, 0), ("KT", hk, 1)] + [("QT", hk % 2, g) for g in range(4)], [("ps", base + j)])

                    def a_softmax(n, hk=hk):
                        base = 3
                        i2 = n % 2
                        S2 = S2b[i2]
                        for g in range(4):
                            slope = 2.0 ** (-8.0 * (hk * 4 + g + 1) / 16.0)
                            P.op("dve", lambda e, S2=S2, g=g, slope=slope, base=base: e.scalar_tensor_tensor(
                                out=S2[:, :, g, :], in0=nd[:], scalar=slope, in1=ps[:, base:base + 3, g * 128:(g + 1) * 128], op0=ALU.mult, op1=ALU.add),
                                reads=[("ps", base), ("ps", base + 1), ("ps", base + 2)], writes=[("S2", i2, g)])
                        PT = PTb[i2]
                        s2r = [("S2", i2, g) for g in range(4)]
                        if tt == 0 and n == 0:
                            parts = [(0, 1, emk[:, 0:1]), (1, 3, None)]
                        elif tt == NT - 1 and n == 3:
                            parts = [(0, 2, None), (2, 3, emk[:, 1:2])]
                        else:
                            parts = [(0, 3, None)]
                        for pi, (j0, j1, bias) in enumerate(parts):
                            if bias is None:
                                P.op("act", lambda e, PT=PT, S2=S2, j0=j0, j1=j1: e.activation(out=PT[:, j0:j1], in_=S2[:, j0:j1], func=AF.Exp),
                                     reads=s2r, writes=[("PT", i2, pi)])
                            else:
                                P.op("act", lambda e, PT=PT, S2=S2, j0=j0, j1=j1, bias=bias: e.activation(out=PT[:, j0:j1], in_=S2[:, j0:j1], func=AF.Exp, bias=bias),
                                     reads=s2r, writes=[("PT", i2, pi)])
                        return [("PT", i2, pi) for pi in range(len(parts))]

                    def a_pv(n, ptr, hk=hk):
                        i2 = n % 2
                        PT = PTb[i2]
                        for j in range(3):
                            kb = n + j
                            mm(ps[:, 6, :], V[:, kb, hk * 128:(hk + 1) * 128], PT[:, j], j == 0, j == 2, ptr + [("V", kb, hk)], [("ps", 6)])
                        for j in range(3):
                            mm(ps[:, 7, :], onesb[:], PT[:, j], j == 0, j == 2, ptr, [("ps", 7)])
                        dn = denb[i2]
                        P.op("dve", lambda e, dn=dn: e.tensor_tensor(out=dn, in0=ps[:, 7, :].rearrange("p (g q) -> p g q", g=4),
                                                                  in1=esink[:, hk * 4:(hk + 1) * 4].unsqueeze(2).broadcast_to([128, 4, 128]), op=ALU.add),
                             reads=[("ps", 7)], writes=[("dn", i2)])
                        P.op("act", lambda e, dn=dn: e.activation(out=dn, in_=dn, func=AF.Ln), reads=[("dn", i2)], writes=[("dn", i2)])
                        P.op("act", lambda e, dn=dn: e.activation(out=dn, in_=dn, func=AF.Exp, scale=-1.0), reads=[("dn", i2)], writes=[("dn", i2)])
                        P.op("dve", lambda e, dn=dn, n=n: e.tensor_tensor(out=attnT[:, hk * 4:(hk + 1) * 4, n * 128:(n + 1) * 128],
                                                                       in0=ps[:, 6, :].rearrange("p (g q) -> p g q", g=4), in1=dn, op=ALU.mult),
                             reads=[("ps", 6), ("dn", i2)], writes=[("attnT", hk, n)])

                    for n in range(4):
                        a_scores(n)
                        ptr = a_softmax(n)
                        if hk < 3:
                            q_chunk(hk + 1, n, kcnt)
                            kcnt += 1
                        a_pv(n, ptr)
                attn_all = [("attnT", hk, n) for hk in range(4) for n in range(4)]
                if tt == 0:
                    dbg("attnT", attnT, [16, 512], BF16)
                if stop_after == "S1":
                    continue
                P.barrier()

                TB = 2176
                hbs = [cv(E + TB * i, [544], F32) for i in range(2)]
                ubs = [cv(E + TB * (2 + i), [544], F32) for i in range(2)]
                t1s = [cv(E + TB * (4 + i), [544], F32) for i in range(2)]
                t2s = [cv(E + TB * (6 + i), [544], F32) for i in range(2)]
                for i in range(16):
                    pb = 4 * (i % 2)
                    k2 = i % 2
                    hb, ub, t1, t2 = hbs[k2], ubs[k2], t1s[k2], t2s[k2]
                    wt, wn = W.get(P, ("in", 24 + i))
                    for kc in range(32):
                        mm(ps[:, pb, :], wt[:, kc, :], xnT[:, kc, 127:639], kc == 0, kc == 31, [wn] + xn_all, [("ps", pb)])
                    for kc in range(32):
                        mm(ps[:, pb + 1, 0:2], wt[:, kc, :], xnT[:, kc, 639:641], kc == 0, kc == 31, [wn] + xn_all, [("ps", pb + 1)])
                    P.op("act", lambda e, hb=hb, pb=pb: e.copy(out=hb[:, 0:512], in_=ps[:, pb, :]), reads=[("ps", pb)], writes=[("hb", k2, 0)])
                    P.op("act", lambda e, hb=hb, pb=pb: e.copy(out=hb[:, 512:514], in_=ps[:, pb + 1, 0:2]), reads=[("ps", pb + 1)], writes=[("hb", k2, 1)])
                    wt, wn = W.get(P, ("in", 56 + i))
                    for kc in range(32):
                        mm(ps[:, pb + 2, :], wt[:, kc, :], xnT[:, kc, 127:639], kc == 0, kc == 31, [wn] + xn_all, [("ps", pb + 2)])
                    for kc in range(32):
                        mm(ps[:, pb + 1, 2:4], wt[:, kc, :], xnT[:, kc, 639:641], kc == 0, kc == 31, [wn] + xn_all, [("ps", pb + 1)])
                    P.op("dve", lambda e, hb=hb, ub=ub, pb=pb: e.tensor_tensor(out=ub[:, 0:512], in0=ps[:, pb + 2, :], in1=hb[:, 0:512], op=ALU.mult),
                         reads=[("ps", pb + 2), ("hb", k2, 0)], writes=[("ub", k2, 0)])
                    P.op("dve", lambda e, hb=hb, ub=ub, pb=pb: e.tensor_tensor(out=ub[:, 512:514], in0=ps[:, pb + 1, 2:4], in1=hb[:, 512:514], op=ALU.mult),
                         reads=[("ps", pb + 1), ("hb", k2, 1)], writes=[("ub", k2, 1)])
                    wt, wn = W.get(P, ("in", 40 + i))
                    for kc in range(32):
                        mm(ps[:, pb + 3, :], wt[:, kc, :], xnT[:, kc, 128:640], kc == 0, kc == 31, [wn] + xn_main, [("ps", pb + 3)])
                    ubr = [("ub", k2, 0), ("ub", k2, 1)]
                    P.op("act", lambda e, ub=ub, t1=t1, i=i: e.mul(out=t1[:, 0:512], in_=ub[:, 0:512], mul=convw[:, 0, i:i + 1]), reads=ubr, writes=[("t1", k2)])
                    P.op("dve", lambda e, ub=ub, t1=t1, t2=t2, i=i: e.scalar_tensor_tensor(out=t2[:, 0:512], in0=ub[:, 1:513], scalar=convw[:, 1, i:i + 1], in1=t1[:, 0:512],
                                                                                      op0=ALU.mult, op1=ALU.add), reads=ubr + [("t1", k2)], writes=[("t2", k2)])
                    P.op("dve", lambda e, ub=ub, t1=t1, t2=t2, i=i: e.scalar_tensor_tensor(out=t1[:, 0:512], in0=ub[:, 2:514], scalar=convw[:, 2, i:i + 1], in1=t2[:, 0:512],
                                                                                      op0=ALU.mult, op1=ALU.add), reads=ubr + [("t2", k2)], writes=[("t1", k2)])
                    P.op("dve", lambda e, t1=t1, i=i, pb=pb: e.tensor_tensor(out=convT[:, i, :], in0=ps[:, pb + 3, :], in1=t1[:, 0:512], op=ALU.mult),
                         reads=[("ps", pb + 3), ("t1", k2)], writes=[("convT", i)])
                conv_all = [("convT", i) for i in range(16)]
                if tt == 0:
                    dbg("convT", convT, [16, 512], BF16)
                if stop_after == "S2":
                    continue

                S3o = E + 8 * TB
                sgas = [cv(S3o + 2048 * i, [512], F32) for i in range(2)]
                sgcs = [cv(S3o + 4096 + 2048 * i, [512], F32) for i in range(2)]
                tas = [cv(S3o + 8192 + 2048 * i, [512], F32) for i in range(2)]
                tbs = [cv(S3o + 12288 + 2048 * i, [512], F32) for i in range(2)]
                for c in range(32):
                    pb = 4 * (c % 2)
                    k2 = c % 2
                    sga, sgc, ta, tb = sgas[k2], sgcs[k2], tas[k2], tbs[k2]
                    wt, wn = W.get(P, ("in", 72 + c))
                    for kc in range(32):
                        mm(ps[:, pb, :], wt[:, kc, :], xnT[:, kc, 128:640], kc == 0, kc == 31, [wn] + xn_main, [("ps", pb)])
                    wt, wn = W.get(P, ("in", 104 + c))
                    for kc in range(32):
                        mm(ps[:, pb + 1, :], wt[:, kc, :], xnT[:, kc, 128:640], kc == 0, kc == 31, [wn] + xn_main, [("ps", pb + 1)])
                    wt, wn = W.get(P, ("oaoc", c))
                    for kc in range(16):
                        mm(ps[:, pb + 2, :], wt[:, kc, :], attnT[:, kc, :], kc == 0, kc == 15, [wn] + attn_all, [("ps", pb + 2)])
                    for kc in range(16):
                        mm(ps[:, pb + 3, :], wt[:, 16 + kc, :], convT[:, kc, :], kc == 0, kc == 15, [wn] + conv_all, [("ps", pb + 3)])
                    P.op("act", lambda e, sga=sga, pb=pb: e.activation(out=sga, in_=ps[:, pb, :], func=AF.Sigmoid), reads=[("ps", pb)], writes=[("sga", k2)])
                    P.op("act", lambda e, sgc=sgc, pb=pb: e.activation(out=sgc, in_=ps[:, pb + 1, :], func=AF.Sigmoid), reads=[("ps", pb + 1)], writes=[("sgc", k2)])
                    P.op("dve", lambda e, sga=sga, ta=ta, pb=pb: e.tensor_tensor(out=ta, in0=ps[:, pb + 2, :], in1=sga, op=ALU.mult),
                         reads=[("ps", pb + 2), ("sga", k2)], writes=[("ta", k2)])
                    P.op("dve", lambda e, sgc=sgc, tb=tb, pb=pb: e.tensor_tensor(out=tb, in0=ps[:, pb + 3, :], in1=sgc, op=ALU.mult),
                         reads=[("ps", pb + 3), ("sgc", k2)], writes=[("tb", k2)])
                    P.op("dve", lambda e, ta=ta, tb=tb, c=c: e.tensor_tensor(out=mT[:, c, :], in0=ta, in1=tb, op=ALU.add),
                         reads=[("ta", k2), ("tb", k2)], writes=[("mT", c)])
                if tt == 0:
                    dbg("mT", mT, [32, 512], BF16)
                if stop_after == "S3":
                    continue
                P.barrier()

                m_all = [("mT", c) for c in range(32)]
                xrs = [cv(E + 2048 * i, [4, 128], F32) for i in range(2)]
                xTs = [cv(E + 4096 + 2048 * i, [512], F32) for i in range(2)]
                for c in range(32):
                    k2 = c % 2
                    xr, xT = xrs[k2], xTs[k2]
                    src = xh[r_tile + 128:r_tile + 640, c * 128:(c + 1) * 128].rearrange("(b t) d -> t b d", t=128)
                    P.dma("sp", lambda e, xr=xr, src=src: e.dma_start(out=xr, in_=src), ("xr", k2), writes=[("xr", k2)])
                    for b in range(4):
                        P.op("pe", lambda e, xr=xr, b=b, k2=k2: e.transpose(out=ps[:, 2 + k2, b * 128:(b + 1) * 128], in_=xr[:, b, :], identity=identf[:]),
                             reads=[("xr", k2)], writes=[("ps", 2 + k2)])
                    P.op("act", lambda e, xT=xT, k2=k2: e.copy(out=xT, in_=ps[:, 2 + k2, :]), reads=[("ps", 2 + k2)], writes=[("xT", k2)])
                    wt, wn = W.get(P, ("out", c))
                    for kc in range(32):
                        mm(ps[:, k2, :], wt[:, kc, :], mT[:, kc, :], kc == 0, kc == 31, [wn] + m_all, [("ps", k2)])
                    P.op("dve", lambda e, xT=xT, k2=k2, c=c: e.tensor_tensor(out=acc[:, c, :], in0=ps[:, k2, :], in1=xT, op=ALU.add),
                         reads=[("ps", k2), ("xT", k2)], writes=[("acc", c)])
                acc_all = [("acc", c) for c in range(32)]
                if tt == 0:
                    dbg("acc1", acc, [32, 512], F32)
                if stop_after == "S4":
                    continue

                sq2s = [cv(E + 8192 + 2048 * i, [512], F32) for i in range(2)]
                rstd2 = cv(E + 12288, [512], F32)
                for c in range(32):
                    sq = sq2s[c % 2]
                    P.op("act", lambda e, sq=sq, c=c: e.activation(out=sq, in_=acc[:, c, :], func=AF.Square), reads=[("acc", c)], writes=[("sq2", c % 2)])
                    mm(ps[:, 4, :], onesf[:], sq, c == 0, c == 31, [("sq2", c % 2)], [("ps", 4)])
                P.op("act", lambda e: e.activation(out=rstd2, in_=ps[:, 4, :], func=AF.Sqrt, bias=epsc[:, 0:1], scale=1.0 / 4096),
                     reads=[("ps", 4)], writes=["rstd2"])
                P.op("dve", lambda e: e.reciprocal(out=rstd2, in_=rstd2), reads=["rstd2"], writes=["rstd2"])
                for c in range(32):
                    P.op("dve", lambda e, c=c: e.scalar_tensor_tensor(out=xn2T[:, c, :], in0=acc[:, c, :], scalar=g2[:, c:c + 1], in1=rstd2, op0=ALU.mult, op1=ALU.mult),
                         reads=[("acc", c), "rstd2"] + m_all, writes=[("xn2T", c)])
                xn2_all = [("xn2T", c) for c in range(32)]
                if tt == 0:
                    dbg("xn2T", xn2T, [32, 512], BF16)
                if stop_after == "S5":
                    continue

                WTs = [cv(E + 16384 * i, [16, 512], BF16) for i in range(2)]
                o6 = E + 16384
                qfs = [cv(o6 + 2048 * i, [512], F32) for i in range(2)]
                Ssbs = [cv(o6 + 4096 + 2048 * i, [4, 128], F32) for i in range(2)]
                wks = [cv(o6 + 8192 + 512 * i, [128], F32) for i in range(4)]
                vv = cv(o6 + 10240, [4, 16, 16], F32)
                idx = cv(o6 + 14336, [4, 16, 16], U32)
                cand = cv(o6 + 18432, [8, 16, 16], F32)
                cwks = [cv(o6 + 26624 + 1024 * i, [256], F32) for i in range(4)]
                tops = cv(o6 + 30720, [8, 16], F32)
                pos = cv(o6 + 31232, [8, 16], U32)
                sm = [cv(o6 + 31744 + 512 * i, [8, 16], F32) for i in range(10)]
                smu = [cv(o6 + 36864 + 512 * i, [8, 16], U32) for i in range(2)]
                es = cv(o6 + 37888, [8], F32)
                eq = cv(OFF_C + 8192, [8, 16, 16], F32)
                rT = cv(OFF_C, [3, 512], F32)
                for cq in range(16):
                    k2 = cq % 2
                    qf, Ssb = qfs[k2], Ssbs[k2]
                    wt, wn = W.get(P, ("qp", cq))
                    for kc in range(32):
                        mm(ps[:, k2, :], wt[:, kc, :], xn2T[:, kc, :], kc == 0, kc == 31, [wn] + xn2_all, [("ps", k2)])
                    P.op("act", lambda e, qf=qf, k2=k2: e.copy(out=qf, in_=ps[:, k2, :]), reads=[("ps", k2)], writes=[("qf", k2)])
                    for b in range(4):
                        mm(ps[:, 2 + k2, b * 128:(b + 1) * 128], qf[:, b * 128:(b + 1) * 128], subk[:, cq, :], True, True, [("qf", k2)], [("ps", 2 + k2)])
                    P.op("act", lambda e, Ssb=Ssb, k2=k2: e.copy(out=Ssb, in_=ps[:, 2 + k2, :].rearrange("p (b n) -> p b n", b=4)), reads=[("ps", 2 + k2)], writes=[("Ssb", k2)])
                    for b in range(4):
                        P.op("dve", lambda e, b=b, cq=cq, Ssb=Ssb: e.max(out=vv[:, b, cq, 0:8], in_=Ssb[:, b, :]), reads=[("Ssb", k2)], writes=[("v8a", b, cq)])
                    for b in range(4):
                        P.op("dve", lambda e, b=b, cq=cq, Ssb=Ssb: e.max_index(out=idx[:, b, cq, 0:8], in_max=vv[:, b, cq, 0:8], in_values=Ssb[:, b, :]),
                             reads=[("Ssb", k2), ("v8a", b, cq)], writes=[("i8a", b, cq)])
                    for b in range(4):
                        P.op("dve", lambda e, b=b, cq=cq, Ssb=Ssb: e.match_replace(out=wks[b], in_to_replace=vv[:, b, cq, 0:8], in_values=Ssb[:, b, :], imm_value=-1e30),
                             reads=[("Ssb", k2), ("v8a", b, cq)], writes=[("wk", b)])
                    for b in range(4):
                        P.op("dve", lambda e, b=b, cq=cq: e.max(out=vv[:, b, cq, 8:16], in_=wks[b]), reads=[("wk", b)], writes=[("v8b", b, cq)])
                    for b in range(4):
                        P.op("dve", lambda e, b=b, cq=cq: e.max_index(out=idx[:, b, cq, 8:16], in_max=vv[:, b, cq, 8:16], in_values=wks[b]),
                             reads=[("wk", b), ("v8b", b, cq)], writes=[("i8b", b, cq)])
                for il in range(16):
                    bk = 5 + il % 2
                    wt, wn = W.get(P, ("dn", il))
                    for kc in range(32):
                        mm(ps[:, bk, :], wt[:, kc, :], xn2T[:, kc, :], kc == 0, kc == 31, [wn] + xn2_all, [("ps", bk)])
                    P.op("act", lambda e, il=il, bk=bk: e.activation(out=WTs[0][:, il, :], in_=ps[:, bk, :], func=AF.Gelu), reads=[("ps", bk)], writes=[("WT", 0, il)])
                hsts = [cv(OFF_C + 6144 + 1024 * i, [512], BF16) for i in range(2)]
                for il in range(16):
                    bk = 5 + il % 2
                    wt, wn = W.get(P, ("dn", 32 + il))
                    for kc in range(32):
                        mm(ps[:, bk, :], wt[:, kc, :], xn2T[:, kc, :], kc == 0, kc == 31, [wn] + xn2_all, [("ps", bk)])
                    P.op("act", lambda e, il=il, bk=bk: e.activation(out=hsts[il % 2], in_=ps[:, bk, :], func=AF.Gelu), reads=[("ps", bk)], writes=[("hst", il % 2)])
                    P.dma("sp", lambda e, il=il, tt=tt: e.dma_start(out=hscr[tt, 0, :, il, :], in_=hsts[il % 2]), ("hs", il % 2), reads=[("hst", il % 2)])
                for b in range(4):
                    vr = [("v8a", b, cq) for cq in range(16)] + [("v8b", b, cq) for cq in range(16)]
                    ir = [("i8a", b, cq) for cq in range(16)] + [("i8b", b, cq) for cq in range(16)]
                    vb4 = vv[:, b].rearrange("p (h two) a -> p h two a", two=2)
                    ib4 = idx[:, b].rearrange("p (h two) a -> p h two a", two=2)
                    v1, v2 = vb4[:, :, 0, :], vb4[:, :, 1, :]
                    P.op("dve", lambda e, v1=v1, v2=v2: e.tensor_tensor(out=cand, in0=v1.unsqueeze(3).broadcast_to([128, 8, 16, 16]),
                                                                     in1=v2.unsqueeze(2).broadcast_to([128, 8, 16, 16]), op=ALU.add), reads=vr, writes=["cand"])
                    for hg in range(2):
                        hs = list(range(hg * 4, hg * 4 + 4))
                        chs = {h: cand[:, h].rearrange("p a b -> p (a b)") for h in hs}
                        for h in hs:
                            P.op("dve", lambda e, h=h, ch=chs[h]: e.max(out=tops[:, h, 0:8], in_=ch), reads=["cand"], writes=[("tops", h, 0)])
                        for h in hs:
                            P.op("dve", lambda e, h=h, ch=chs[h]: e.max_index(out=pos[:, h, 0:8], in_max=tops[:, h, 0:8], in_values=ch), reads=["cand", ("tops", h, 0)], writes=[("pos", h, 0)])
                        for h in hs:
                            P.op("dve", lambda e, h=h, ch=chs[h]: e.match_replace(out=cwks[h % 4], in_to_replace=tops[:, h, 0:8], in_values=ch, imm_value=-1e30),
                                 reads=["cand", ("tops", h, 0)], writes=[("cw", h % 4)])
                        for h in hs:
                            P.op("dve", lambda e, h=h: e.max(out=tops[:, h, 8:16], in_=cwks[h % 4]), reads=[("cw", h % 4)], writes=[("tops", h, 1)])
                        for h in hs:
                            P.op("dve", lambda e, h=h: e.max_index(out=pos[:, h, 8:16], in_max=tops[:, h, 8:16], in_values=cwks[h % 4]), reads=[("cw", h % 4), ("tops", h, 1)], writes=[("pos", h, 1)])
                    tr = [("tops", h, k) for h in range(8) for k in range(2)]
                    pr = [("pos", h, k) for h in range(8) for k in range(2)]
                    dd, ee, gg, paf, pbf, i1f, i2f, ia, jb, rs_ = sm
                    pa, pb_ = smu
                    P.op("dve", lambda e: e.tensor_tensor(out=dd, in0=tops, in1=tops[:, :, 0:1].broadcast_to([128, 8, 16]), op=ALU.subtract), reads=tr, writes=["dd"])
                    P.op("act", lambda e: e.activation(out=ee, in_=dd, func=AF.Exp), reads=["dd"], writes=["ee"])
                    P.op("dve", lambda e: e.tensor_single_scalar(out=pa, in_=pos, scalar=4, op=ALU.logical_shift_right), reads=pr, writes=["pa"])
                    P.op("dve", lambda e: e.tensor_single_scalar(out=pb_, in_=pos, scalar=15, op=ALU.bitwise_and), reads=pr, writes=["pb"])
                    P.op("dve", lambda e, ib4=ib4: e.tensor_copy(out=i1f, in_=ib4[:, :, 0, :]), reads=ir, writes=["i1f"])
                    P.op("dve", lambda e, ib4=ib4: e.tensor_copy(out=i2f, in_=ib4[:, :, 1, :]), reads=ir, writes=["i2f"])
                    P.op("dve", lambda e: e.tensor_copy(out=paf, in_=pa), reads=["pa"], writes=["paf"])
                    P.op("dve", lambda e: e.tensor_copy(out=pbf, in_=pb_), reads=["pb"], writes=["pbf"])
                    P.op("dve", lambda e: e.tensor_reduce(out=es, in_=ee, axis=AX.X, op=ALU.add), reads=["ee"], writes=["es"])
                    P.op("dve", lambda e: e.reciprocal(out=es, in_=es), reads=["es"], writes=["es"])
                    P.op("dve", lambda e: e.tensor_tensor(out=gg, in0=ee, in1=es.unsqueeze(2).broadcast_to([128, 8, 16]), op=ALU.mult), reads=["ee", "es"], writes=["gg"])
                    iota16 = iota[:, 0:16].unsqueeze(1).unsqueeze(1).broadcast_to([128, 8, 16, 16])
                    for (pf, ixf, dst, nm) in ((paf, i1f, ia, "ia"), (pbf, i2f, jb, "jb")):
                        P.op("dve", lambda e, pf=pf: e.tensor_tensor(out=eq, in0=pf.unsqueeze(3).broadcast_to([128, 8, 16, 16]), in1=iota16, op=ALU.is_equal),
                             reads=["paf", "pbf"], writes=["eq"])
                        P.op("dve", lambda e, ixf=ixf: e.tensor_tensor(out=eq, in0=eq, in1=ixf.unsqueeze(2).broadcast_to([128, 8, 16, 16]), op=ALU.mult),
                             reads=["eq", "i1f", "i2f"], writes=["eq"])
                        P.op("dve", lambda e, dst=dst: e.tensor_reduce(out=dst, in_=eq, axis=AX.X, op=ALU.add), reads=["eq"], writes=[nm])
                    for k, (src_, nm) in enumerate(((ia, "ia"), (jb, "jb"), (gg, "gg"))):
                        P.op("pe", lambda e, src_=src_, k=k: e.transpose(out=ps[:, 4, k * 128:(k + 1) * 128], in_=src_.rearrange("p h k -> p (h k)"), identity=identf[:]),
                             reads=[nm], writes=[("ps", 4)])
                    P.op("act", lambda e, b=b: e.copy(out=rT[:, :, b * 128:(b + 1) * 128], in_=ps[:, 4, 0:384].rearrange("p (k t) -> p k t", k=3)),
                         reads=[("ps", 4)], writes=[("rT", b)])
                if tt == 0:
                    dbg("rT", rT, [3, 512], F32)
                if stop_after == "S6a":
                    continue
                P.barrier()

                L = cv(E + 32768, [32, 128], BF16)
                R = cv(E + 40960, [32, 128], BF16)
                Gsbs = [cv(E + 49152, [128, 32], BF16), cv(OFF_C + 8192, [128, 32], BF16)]
                iota_b = iota[:].unsqueeze(1).broadcast_to([128, 32, 128])
                for qb in range(16):
                    k2 = qb % 2
                    t0 = qb * 32
                    Gsb = Gsbs[k2]
                    jbv = rT[:, 1, t0:t0 + 32].unsqueeze(2).broadcast_to([128, 32, 128])
                    ggv = rT[:, 2, t0:t0 + 32].unsqueeze(2).broadcast_to([128, 32, 128])
                    iav = rT[:, 0, t0:t0 + 32].unsqueeze(2).broadcast_to([128, 32, 128])
                    P.op("dve", lambda e, jbv=jbv: e.tensor_tensor(out=L, in0=iota_b, in1=jbv, op=ALU.is_equal), writes=["L"])
                    P.op("dve", lambda e, ggv=ggv: e.tensor_tensor(out=L, in0=L, in1=ggv, op=ALU.mult), reads=["L"], writes=["L"])
                    P.op("dve", lambda e, iav=iav: e.tensor_tensor(out=R, in0=iota_b, in1=iav, op=ALU.is_equal), writes=["R"])
                    for t in range(32):
                        bk = 5 + (t // 4) % 2
                        mm(ps[:, bk, (t % 4) * 128:(t % 4 + 1) * 128], L[:, t, :], R[:, t, :], True, True, ["L", "R"], [("ps", bk)])
                        if t % 4 == 3:
                            src_ = ps[:, bk, :].rearrange("p (t i) -> p i t", t=4)
                            dst_ = Gsb[:, :, t - 3:t + 1]
                            P.op("act", lambda e, src_=src_, dst_=dst_: e.copy(out=dst_, in_=src_), reads=[("ps", bk)], writes=[("Gsb", k2, t // 4)])
                    P.dma("sp", lambda e, Gsb=Gsb, qb=qb, tt=tt: e.dma_start(out=gscr[tt, qb], in_=Gsb), ("gs", k2),
                          reads=[("Gsb", k2, u) for u in range(8)], writes=[("gscr", qb)])
                    bk = qb % 2
                    wt, wn = W.get(P, ("dn", 16 + qb))
                    for kc in range(32):
                        mm(ps[:, bk, :], wt[:, kc, :], xn2T[:, kc, :], kc == 0, kc == 31, [wn], [("ps", bk)])
                    P.op("act", lambda e, qb=qb, bk=bk: e.activation(out=WTs[1][:, qb, :], in_=ps[:, bk, :], func=AF.Gelu), reads=[("ps", bk)], writes=[("WT", 1, qb)])
                    bk = 2 + qb % 2
                    wt, wn = W.get(P, ("dn", 48 + qb))
                    for kc in range(32):
                        mm(ps[:, bk, :], wt[:, kc, :], xn2T[:, kc, :], kc == 0, kc == 31, [wn], [("ps", bk)])
                    P.op("act", lambda e, qb=qb, bk=bk: e.activation(out=hsts[qb % 2], in_=ps[:, bk, :], func=AF.Gelu), reads=[("ps", bk)], writes=[("hst", qb % 2)])
                    P.dma("sp", lambda e, qb=qb, tt=tt: e.dma_start(out=hscr[tt, 1, :, qb, :], in_=hsts[qb % 2]), ("hs", qb % 2), reads=[("hst", qb % 2)])
                if stop_after == "S6b":
                    continue
                P.barrier()

                Gcs = [cv(E + 32768 + 8192 * i, [16, 8, 32], BF16) for i in range(2)]
                gls = [cv(E + 49152 + 1024 * i, [512], BF16) for i in range(2)]
                for s in range(8):
                    WT = WTs[s % 2]
                    for gi in range(2):
                        i0 = s * 16 + gi * 8
                        src_ = gscr[tt, :, :, i0:i0 + 8, :].rearrange("q j i t -> j q i t")
                        P.dma("sp", lambda e, gi=gi, src_=src_: e.dma_start(out=Gcs[gi], in_=src_), ("gc", gi), writes=[("Gc", gi)])
                    if s in (2, 3):
                        P.dma("sp", lambda e, WT=WT, s=s, tt=tt: e.dma_start(out=WT, in_=hscr[tt, s - 2]), ("hl", s % 2),
                              writes=[("WT", s % 2, il) for il in range(16)])
                    for il in range(16):
                        i = s * 16 + il
                        k2 = il % 2
                        if s < 4:
                            P.op("dve", lambda e, WT=WT, il=il: e.tensor_tensor(out=WT[:, il, :].rearrange("p (q t) -> p q t", q=16), in0=WT[:, il, :].rearrange("p (q t) -> p q t", q=16),
                                                                            in1=Gcs[il // 8][:, :, il % 8, :], op=ALU.mult),
                                 reads=[("Gc", il // 8), ("WT", s % 2, il)], writes=[("WT", s % 2, il)])
                            continue
                        gl = gls[k2]
                        wt, wn = W.get(P, ("dn", i))
                        for kc in range(32):
                            mm(ps[:, k2, :], wt[:, kc, :], xn2T[:, kc, :], kc == 0, kc == 31, [wn], [("ps", k2)])
                        P.op("act", lambda e, gl=gl, k2=k2: e.activation(out=gl, in_=ps[:, k2, :], func=AF.Gelu), reads=[("ps", k2)], writes=[("gl", k2)])
                        P.op("dve", lambda e, gl=gl, WT=WT, il=il: e.tensor_tensor(out=WT[:, il, :].rearrange("p (q t) -> p q t", q=16), in0=gl.rearrange("p (q t) -> p q t", q=16),
                                                                                in1=Gcs[il // 8][:, :, il % 8, :], op=ALU.mult),
                             reads=[("gl", k2), ("Gc", il // 8)], writes=[("WT", s % 2, il)])
                    wtr = [("WT", s % 2, il) for il in range(16)]
                    for c in range(32):
                        bk = 2 + c % 2
                        if c % 2 == 0:
                            wt, wn = W.get(P, ("up", s, c // 2))
                        for il in range(16):
                            mm(ps[:, bk, :], wt[:, (c % 2) * 16 + il, :], WT[:, il, :], il == 0, il == 15, [wn] + wtr, [("ps", bk)])
                        P.op("dve", lambda e, c=c, bk=bk: e.tensor_tensor(out=acc[:, c, :], in0=ps[:, bk, :], in1=acc[:, c, :], op=ALU.add),
                             reads=[("ps", bk), ("acc", c)], writes=[("acc", c)])
                if tt == 0:
                    dbg("acc2", acc, [32, 512], F32)
                P.barrier()

                ots = [cv(OFF_B + 16384 * i, [4096], F32) for i in range(2)]
                ev = 0
                for b in range(4):
                    ot = ots[b % 2]
                    for g8 in range(8):
                        bk = g8 % 4
                        for q in range(4):
                            c = g8 * 4 + q
                            P.op("pe", lambda e, c=c, b=b, bk=bk, q=q: e.transpose(out=ps[:, bk, q * 128:(q + 1) * 128], in_=acc[:, c, b * 128:(b + 1) * 128], identity=identf[:]),
                                 reads=[("acc", c)], writes=[("ps", bk)])
                        if ev % 2 == 0:
                            P.op("act", lambda e, ot=ot, g8=g8, bk=bk: e.copy(out=ot[:, g8 * 512:(g8 + 1) * 512], in_=ps[:, bk, :]), reads=[("ps", bk)], writes=[("ot", b % 2, g8)])
                        else:
                            P.op("dve", lambda e, ot=ot, g8=g8, bk=bk: e.tensor_copy(out=ot[:, g8 * 512:(g8 + 1) * 512], in_=ps[:, bk, :]), reads=[("ps", bk)], writes=[("ot", b % 2, g8)])
                        ev += 1
                    r0 = tt * 512 + b * 128
                    P.dma("sp", lambda e, ot=ot, r0=r0: e.dma_start(out=out_d[r0:r0 + 128, :], in_=ot), ("ot", b % 2),
                          reads=[("ot", b % 2, g8) for g8 in range(8)])
                if tt == NT - 1:
                    P.barrier()

        class WStream:
            def __init__(self, keys=None):
                self.record = keys is None
                self.keys = [] if keys is None else keys
                self.pos = 0
                self.issued = 0

            def _issue(self, P, i):
                key = self.keys[i]
                src, kcn = wdram[key]
                slot = i % NB_W
                P.dma("pool", lambda e, src=src, kcn=kcn, slot=slot: e.dma_start(out=wbuf[slot][:, 0:kcn, :], in_=src), ("w", slot), writes=[("wb", slot)])

            def get(self, P, key):
                if self.record:
                    self.keys.append(key)
                    return wbuf[0], ("wb", 0)
                i = self.pos
                assert self.keys[i] == key, (self.keys[i], key)
                while self.issued < min(len(self.keys), i + NB_W):
                    self._issue(P, self.issued)
                    self.issued += 1
                self.pos += 1
                return wbuf[i % NB_W], ("wb", i % NB_W)

        Wr = WStream()
        body(Prog(nc), Wr)
        P = Prog(nc)
        body(P, WStream(Wr.keys))
        P.emit()
        info = {"nops": len(P.ops), "stats": P.stats, "nchunks": len(Wr.keys)}
    return nc, info, list(dbg_tensors.keys())


def _chunked(w, kcn):
    K, N = w.shape
    return np.ascontiguousarray(w.reshape(K // 128, 128, N // 128, 128).transpose(2, 1, 0, 3))


def prepare_inputs(x, norm1_g, w_in, q_norm_g, k_norm_g, sink_logits, conv_w, w_o_attn, w_o_conv,
                   w_out, norm2_g, w_q_peer, sub_keys, w_down, w_up):
    f = lambda a: np.asarray(a, dtype=np.float32)
    x = f(x)
    shared = {}
    shared["w_in_r"] = _chunked(f(w_in)[0], 32)
    shared["w_oaoc_r"] = np.ascontiguousarray(np.concatenate([_chunked(f(w_o_attn)[0], 16), _chunked(f(w_o_conv)[0], 16)], axis=2))
    shared["w_out_r"] = _chunked(f(w_out)[0], 32)
    shared["w_qp_r"] = _chunked(f(w_q_peer)[0], 32)
    wd = f(w_down)[0]
    shared["w_dn_r"] = np.ascontiguousarray(wd.reshape(128, 128, 32, 128).transpose(0, 3, 2, 1))
    wu = f(w_up)[0]
    shared["w_up_r"] = np.ascontiguousarray(wu.reshape(8, 16, 128, 16, 2, 128).transpose(0, 3, 2, 4, 1, 5)).reshape(8, 16, 128, 32, 128)
    shared["g1"] = np.ascontiguousarray(f(norm1_g)[0].reshape(32, 128).T)
    shared["g2"] = np.ascontiguousarray(f(norm2_g)[0].reshape(32, 128).T)
    shared["qkg"] = np.ascontiguousarray(np.stack([f(q_norm_g)[0], f(k_norm_g)[0]], axis=1))
    shared["sinkb"] = np.ascontiguousarray(np.broadcast_to(f(sink_logits)[0][None, :], (128, 16)))
    shared["convw"] = np.ascontiguousarray(f(conv_w)[0].reshape(3, 16, 128).transpose(2, 0, 1))
    sk = f(sub_keys)[0]
    shared["subk"] = np.ascontiguousarray(sk.reshape(16, 128, 128).transpose(2, 0, 1))
    shared["ident"] = np.eye(128, dtype=np.float32)
    s_ = np.arange(128)[:, None, None]
    j_ = np.arange(3)[None, :, None]
    q_ = np.arange(128)[None, None, :]
    dist = np.abs((j_ - 1) * 128 + s_ - q_)
    shared["nd"] = np.where(dist <= 128, -dist.astype(np.float32), np.float32(-1.0e7)).astype(np.float32)
    shared["iota"] = np.ascontiguousarray(np.broadcast_to(np.arange(128, dtype=np.float32)[None, :], (128, 128)))
    in_maps = []
    for c in range(NCORES):
        b, hf = c // 2, c % 2
        xhh = np.zeros((2304, 4096), np.float32)
        lo = hf * 2048 - 128
        hi = hf * 2048 + 2048 + 128
        slo, shi = max(lo, 0), min(hi, 4096)
        xhh[slo - lo:shi - lo] = x[b, slo:shi]
        em = np.zeros((128, 2), np.float32)
        if hf == 0:
            em[:, 0] = -30000.0
        else:
            em[:, 1] = -30000.0
        m = dict(shared)
        m["xh"] = xhh
        m["emask"] = em
        in_maps.append(m)
    return in_maps


_CACHE = {}


def kernel(**inputs):
    in_maps = prepare_inputs(**inputs)
    if "nc" not in _CACHE:
        _CACHE["nc"] = build_program(NT=4, debug=False)[0]
    nc = _CACHE["nc"]
    res = run_bass_kernel_spmd(nc, in_maps, core_ids=list(range(NCORES)))
    out = np.empty((4, 4096, 4096), np.float32)
    for c in range(NCORES):
        b, hf = c // 2, c % 2
        out[b, hf * 2048:(hf + 1) * 2048] = res.results[c]["out"]
    return out
```

```python
import contextlib
import math
import os
import numpy as np
import concourse.bass as bass
import concourse.mybir as mybir
from concourse.bass_utils import run_bass_kernel_spmd

F32 = mybir.dt.float32
BF16 = mybir.dt.bfloat16
U32 = mybir.dt.uint32
AF = mybir.ActivationFunctionType
ALU = mybir.AluOpType
AX = mybir.AxisListType

EPS = 1e-6
NCORES = 8
ENGS = ("pe", "act", "dve", "pool", "sp")


class Op:
    __slots__ = ("eng", "fn", "deps", "is_dma", "semkey", "tick", "marked", "idx")

    def __init__(self, eng, fn, is_dma=False, semkey=None):
        self.eng = eng
        self.fn = fn
        self.deps = []
        self.is_dma = is_dma
        self.semkey = semkey
        self.tick = None
        self.marked = False


class Prog:
    def __init__(self, nc):
        self.nc = nc
        self.ops = []
        self.res = {}
        self.dma_counts = {}
        self.last = {}
        self.open_dmas = []

    def _add(self, op, reads, writes):
        op.idx = len(self.ops)
        deps = set()
        for r in reads:
            st = self.res.get(r)
            if st is not None:
                deps.update(st[0])
        for w in writes:
            st = self.res.get(w)
            if st is not None:
                deps.update(st[0])
                deps.update(st[1])
        for r in reads:
            st = self.res.setdefault(r, [[], []])
            st[1].append(op.idx)
        for w in writes:
            self.res[w] = [[op.idx], []]
        deps.discard(op.idx)
        op.deps = deps
        self.ops.append(op)
        if op.is_dma:
            self.open_dmas.append(op.idx)
        elif op.fn is not None:
            self.last[op.eng] = op.idx
        return op

    def op(self, eng, fn, reads=(), writes=()):
        return self._add(Op(eng, fn), reads, writes)

    def dma(self, eng, fn, semkey, reads=(), writes=()):
        op = Op(eng, fn, is_dma=True, semkey=semkey)
        self.dma_counts[semkey] = self.dma_counts.get(semkey, 0) + 1
        op.tick = 16 * self.dma_counts[semkey]
        return self._add(op, reads, writes)

    def barrier(self):
        deps = set(self.last.values()) | set(self.open_dmas)
        self.open_dmas = []
        for e in ENGS:
            o = Op(e, None)
            o.idx = len(self.ops)
            o.deps = set(deps)
            self.ops.append(o)
        self.res = {}

    def emit(self):
        nc = self.nc
        ops = self.ops
        for o in ops:
            latest = {}
            keep = []
            for d in o.deps:
                p = ops[d]
                if p.is_dma:
                    keep.append(d)
                else:
                    if p.eng == o.eng and not o.is_dma and p.eng == "pe":
                        continue
                    if p.eng not in latest or latest[p.eng] < d:
                        latest[p.eng] = d
            keep.extend(latest.values())
            o.deps = sorted(keep)
            for d in o.deps:
                ops[d].marked = True
        cnt = {e: 0 for e in ENGS}
        for o in ops:
            if o.is_dma or o.fn is None:
                continue
            if o.marked:
                cnt[o.eng] += 1
                o.tick = cnt[o.eng]
        self.stats = dict(cnt)
        streams = {e: [o for o in ops if o.eng == e] for e in ENGS}
        stack = contextlib.ExitStack()
        with stack:
            esem = {e: stack.enter_context(nc.semaphore("S_" + e)) for e in ENGS}
            dsem = {}
            for n, k in enumerate(self.dma_counts):
                dsem[k] = stack.enter_context(nc.semaphore("D%d" % n))
            block = stack.enter_context(nc.Block())

            def run_stream(e, engobj):
                waited = {}
                for o in streams[e]:
                    for d in o.deps:
                        p = ops[d]
                        if p.is_dma:
                            key = ("d", p.semkey)
                            sem = dsem[p.semkey]
                        else:
                            key = ("e", p.eng)
                            sem = esem[p.eng]
                        if waited.get(key, 0) >= p.tick:
                            continue
                        engobj.wait_ge(sem, p.tick)
                        waited[key] = p.tick
                    if o.fn is None:
                        continue
                    ins = o.fn(engobj)
                    if o.is_dma:
                        ins.then_inc(dsem[o.semkey], 16)
                    elif o.marked:
                        ins.then_inc(esem[o.eng], 1)
                last = {}
                for o in streams[e]:
                    if o.is_dma:
                        last[o.semkey] = max(last.get(o.semkey, 0), o.tick)
                for k, v in last.items():
                    if waited.get(("d", k), 0) < v:
                        engobj.wait_ge(dsem[k], v)

            @block.tensor
            def _(eng):
                run_stream("pe", eng)

            @block.scalar
            def _(eng):
                run_stream("act", eng)

            @block.vector
            def _(eng):
                run_stream("dve", eng)

            @block.gpsimd
            def _(eng):
                run_stream("pool", eng)

            @block.sync
            def _(eng):
                run_stream("sp", eng)


NB_W = 3
OFF_A = 0
OFF_B = 65536
OFF_C = 98304
OFF_D = 114688
OFF_E = 139264
E_SIZE = 57344
ARENA = OFF_E + E_SIZE


def build_program(NT=4, debug=False, stop_after=None):
    nc = bass.Bass("TRN2", target_bir_lowering=False)
    dt_in = lambda name, shape, dt=F32: nc.dram_tensor(name, list(shape), dt, kind="ExternalInput").ap()
    xh = dt_in("xh", [2304, 4096])
    emask = dt_in("emask", [128, 2])
    w_in_r = dt_in("w_in_r", [136, 128, 32, 128])
    w_oaoc_r = dt_in("w_oaoc_r", [32, 128, 32, 128])
    w_out_r = dt_in("w_out_r", [32, 128, 32, 128])
    w_qp_r = dt_in("w_qp_r", [16, 128, 32, 128])
    w_dn_r = dt_in("w_dn_r", [128, 128, 32, 128])
    w_up_r = dt_in("w_up_r", [8, 16, 128, 32, 128])
    g1_d = dt_in("g1", [128, 32])
    g2_d = dt_in("g2", [128, 32])
    qkg_d = dt_in("qkg", [128, 2])
    sinkb_d = dt_in("sinkb", [128, 16])
    convw_d = dt_in("convw", [128, 3, 16])
    subk_d = dt_in("subk", [128, 16, 128])
    ident_d = dt_in("ident", [128, 128])
    nd_d = dt_in("nd", [128, 3, 128])
    iota_d = dt_in("iota", [128, 128])
    out_d = nc.dram_tensor("out", [2048, 4096], F32, kind="ExternalOutput").ap()
    gscr = nc.dram_tensor("gscr", [NT, 16, 128, 128, 32], BF16).ap()
    hscr = nc.dram_tensor("hscr", [NT, 2, 128, 16, 512], BF16).ap()
    dbg_tensors = {}

    wdram = {}
    for c in range(136):
        wdram[("in", c)] = (w_in_r[c], 32)
    for c in range(32):
        wdram[("oaoc", c)] = (w_oaoc_r[c], 32)
        wdram[("out", c)] = (w_out_r[c], 32)
    for c in range(16):
        wdram[("qp", c)] = (w_qp_r[c], 32)
    for i in range(128):
        wdram[("dn", i)] = (w_dn_r[i], 32)
    for s in range(8):
        for c in range(16):
            wdram[("up", s, c)] = (w_up_r[s, c], 32)

    st = contextlib.ExitStack()
    with st:
        sbt = lambda name, shape, dt: st.enter_context(nc.sbuf_tensor(name, list(shape), dt))
        arena = sbt("arena", [128, ARENA // 2], BF16)
        ps = st.enter_context(nc.psum_tensor("ps", [128, 8, 512], F32))
        identf = sbt("identf", [128, 128], F32)
        onesf = sbt("onesf", [128, 128], F32)
        onesb = sbt("onesb", [128, 128], BF16)
        nd = sbt("ndt", [128, 3, 128], F32)
        iota = sbt("iotat", [128, 128], F32)
        g1 = sbt("g1t", [128, 32], F32)
        g2 = sbt("g2t", [128, 32], F32)
        qkg = sbt("qkgt", [128, 2], F32)
        kgs = sbt("kgs", [128, 1], F32)
        sinkb = sbt("sinkbt", [128, 16], F32)
        esink = sbt("esink", [128, 16], F32)
        convw = sbt("convwt", [128, 3, 16], F32)
        subk = sbt("subkt", [128, 16, 128], F32)
        emk = sbt("emk", [128, 2], F32)
        epsc = sbt("epsc", [128, 2], F32)

        def cv(off, shape, dt):
            n = 1
            for s_ in shape:
                n *= s_
            nbytes = n * (2 if dt == BF16 else 4)
            assert off % 4 == 0
            a = arena[:, off // 2:(off + nbytes) // 2]
            if dt != BF16:
                a = a.bitcast(dt)
            if len(shape) == 2:
                a = a.rearrange("p (a b) -> p a b", a=shape[0])
            elif len(shape) == 3:
                a = a.rearrange("p (a b c) -> p a b c", a=shape[0], b=shape[1])
            elif len(shape) == 4:
                a = a.rearrange("p (a b c d) -> p a b c d", a=shape[0], b=shape[1], c=shape[2])
            return a

        E = OFF_E
        xnT = cv(OFF_A, [32, 768], BF16)
        attnT = cv(OFF_A + 49152, [16, 512], BF16)
        acc = cv(OFF_A, [32, 512], F32)
        mT = cv(OFF_B, [32, 512], BF16)
        xn2T = cv(OFF_B, [32, 512], BF16)
        convT = cv(OFF_C, [16, 512], BF16)
        wbuf = [cv(OFF_D + 8192 * i, [32, 128], BF16) for i in range(NB_W)]

        def body(P, W):
            def dbg(name, src_ap, shape, dt, reads=()):
                if not debug:
                    return
                if name not in dbg_tensors:
                    dbg_tensors[name] = nc.dram_tensor("dbg_" + name, [128] + list(shape), dt, kind="ExternalOutput").ap()
                d = dbg_tensors[name]
                P.barrier()
                P.dma("sp", lambda e: e.dma_start(out=d, in_=src_ap), ("dbg", name))
                P.barrier()

            def mm(out, lhsT, rhs, start, stop, reads, writes):
                P.op("pe", lambda e: e.matmul(out=out, lhsT=lhsT, rhs=rhs, start=start, stop=stop), reads, writes)

            for i, (t, d) in enumerate(((identf, ident_d), (nd, nd_d), (iota, iota_d), (g1, g1_d), (g2, g2_d), (qkg, qkg_d),
                                        (sinkb, sinkb_d), (convw, convw_d), (emk, emask))):
                P.dma("sp", lambda e, t=t, d=d: e.dma_start(out=t[:], in_=d), ("c", i), writes=[("const", i)])
            P.dma("sp", lambda e: e.dma_start(out=subk[:], in_=subk_d), ("c", 20), writes=[("const", 20)])
            P.op("dve", lambda e: e.memset(onesf[:], 1.0), writes=[("const", 30)])
            P.op("dve", lambda e: e.memset(onesb[:], 1.0), writes=[("const", 31)])
            P.op("dve", lambda e: e.memset(epsc[:, 0:1], EPS), writes=[("const", 34)])
            P.op("dve", lambda e: e.memset(epsc[:, 1:2], 128.0 * EPS), writes=[("const", 35)])
            P.op("dve", lambda e: e.tensor_scalar(out=kgs[:], in0=qkg[:, 1:2], scalar1=math.sqrt(128.0), scalar2=None, op0=ALU.mult),
                 reads=[("const", 5)], writes=[("const", 32)])
            P.op("act", lambda e: e.activation(out=esink[:], in_=sinkb[:], func=AF.Exp), reads=[("const", 6)], writes=[("const", 33)])
            P.barrier()

            def qknorm(src, n, gain, out, k, reads, writes, sbank=6):
                sq = cv(E + 16384 + 2048 * (k % 2), [512], F32)
                rs = cv(E + 20480 + 2048 * (k % 2), [512], F32)
                P.op("act", lambda e: e.activation(out=sq[:, 0:n], in_=src, func=AF.Square), reads=reads, writes=[("sqk", k % 2)])
                mm(ps[:, sbank, 0:n], onesf[:], sq[:, 0:n], True, True, [("sqk", k % 2)], [("ps", sbank)])
                P.op("act", lambda e: e.activation(out=rs[:, 0:n], in_=ps[:, sbank, 0:n], func=AF.Ln, bias=epsc[:, 1:2], scale=1.0),
                     reads=[("ps", sbank)], writes=[("rsk", k % 2)])
                P.op("act", lambda e: e.activation(out=rs[:, 0:n], in_=rs[:, 0:n], func=AF.Exp, scale=-0.5), reads=[("rsk", k % 2)], writes=[("rsk", k % 2)])
                P.op("dve", lambda e: e.scalar_tensor_tensor(out=out, in0=src, scalar=gain, in1=rs[:, 0:n], op0=ALU.mult, op1=ALU.mult),
                     reads=list(reads) + [("rsk", k % 2)], writes=writes)

            for tt in range(NT):
                r_tile = tt * 512
                xs_bufs = [cv(OFF_C, [4096], F32), cv(E, [4096], F32)]
                xTts = [cv(E + 16384, [32, 128], F32), cv(E + 37376, [32, 128], F32)]
                sq_bufs = [cv(E + 32768, [512], F32), cv(E + 34816, [512], F32)]
                rstd = cv(E + 36864, [128], F32)
                for wb in range(6):
                    xs = xs_bufs[wb % 2]
                    xTt = xTts[wb % 2]
                    xw2 = wb % 2
                    xsn = ("xs", wb % 2)
                    r0 = r_tile + wb * 128
                    P.dma("sp", lambda e, xs=xs, r0=r0: e.dma_start(out=xs, in_=xh[r0:r0 + 128, :]), xsn, writes=[xsn])
                    for g8 in range(8):
                        bk = g8 % 4
                        for q in range(4):
                            c = g8 * 4 + q
                            P.op("pe", lambda e, xs=xs, c=c, bk=bk, q=q: e.transpose(out=ps[:, bk, q * 128:(q + 1) * 128], in_=xs[:, c * 128:(c + 1) * 128], identity=identf[:]),
                                 reads=[xsn], writes=[("ps", bk)])
                        sq = sq_bufs[g8 % 2]
                        P.op("act", lambda e, sq=sq, bk=bk: e.activation(out=sq, in_=ps[:, bk, :], func=AF.Square), reads=[("ps", bk)], writes=[("sq", g8 % 2)])
                        for q in range(4):
                            c = g8 * 4 + q
                            P.op("act", lambda e, c=c, bk=bk, q=q, xTt=xTt: e.mul(out=xTt[:, c, :], in_=ps[:, bk, q * 128:(q + 1) * 128], mul=g1[:, c:c + 1]),
                                 reads=[("ps", bk)], writes=[("xTt", xw2, c)])
                        for q in range(4):
                            c = g8 * 4 + q
                            mm(ps[:, 7, 0:128], onesf[:], sq[:, q * 128:(q + 1) * 128], c == 0, c == 31, [("sq", g8 % 2)], [("ps", 7)])
                    P.op("act", lambda e: e.activation(out=rstd, in_=ps[:, 7, 0:128], func=AF.Sqrt, bias=epsc[:, 0:1], scale=1.0 / 4096),
                         reads=[("ps", 7)], writes=["rstd"])
                    P.op("dve", lambda e: e.reciprocal(out=rstd, in_=rstd), reads=["rstd"], writes=["rstd"])
                    P.op("dve", lambda e, wb=wb, xTt=xTt: e.tensor_tensor(out=xnT[:, :, wb * 128:(wb + 1) * 128], in0=xTt, in1=rstd.unsqueeze(1).broadcast_to([128, 32, 128]), op=ALU.mult),
                         reads=[("xTt", xw2, c) for c in range(32)] + ["rstd"], writes=[("xnT", wb)] + [("acc", c) for c in range(32)])
                xn_all = [("xnT", wb) for wb in range(6)]
                xn_main = [("xnT", wb) for wb in range(1, 5)]
                if tt == 0:
                    dbg("xnT", xnT, [32, 768], BF16)
                if stop_after == "S0":
                    continue

                KT = cv(E, [4, 768], BF16)
                V = cv(E + 6144, [6, 512], BF16)
                QTs = [cv(E + 12288, [4, 512], BF16), cv(E + 47104, [4, 512], BF16)]
                S2b = [cv(E + 24576 + 6144 * i, [3, 4, 128], F32) for i in range(2)]
                PTb = [cv(E + 36864 + 3072 * i, [3, 4, 128], BF16) for i in range(2)]
                denb = [cv(E + 43008 + 2048 * i, [4, 128], F32) for i in range(2)]
                kcnt = 0
                for hk in range(4):
                    wt, wn = W.get(P, ("in", 16 + hk))
                    for half, (w0, n) in enumerate(((0, 512), (512, 256))):
                        bk = half
                        for kc in range(32):
                            mm(ps[:, bk, 0:n], wt[:, kc, :], xnT[:, kc, w0:w0 + n], kc == 0, kc == 31, [wn] + xn_all, [("ps", bk)])
                        qknorm(ps[:, bk, 0:n], n, kgs[:, 0:1], KT[:, hk, w0:w0 + n], kcnt, [("ps", bk)], [("KT", hk, half)])
                        kcnt += 1
                for hk in range(4):
                    wt, wn = W.get(P, ("in", 20 + hk))
                    for wb in range(6):
                        bk = 2 + (wb % 2)
                        for kc in range(32):
                            mm(ps[:, bk, 0:128], xnT[:, kc, wb * 128:(wb + 1) * 128], wt[:, kc, :], kc == 0, kc == 31, [wn, ("xnT", wb)], [("ps", bk)])
                        P.op("act", lambda e, wb=wb, hk=hk, bk=bk: e.copy(out=V[:, wb, hk * 128:(hk + 1) * 128], in_=ps[:, bk, 0:128]),
                             reads=[("ps", bk)], writes=[("V", wb, hk)])
                if tt == 0:
                    dbg("KT", KT, [4, 768], BF16)
                    dbg("V", V, [6, 512], BF16)
                def q_chunk(hk, g, kq):
                    wt, wn = W.get(P, ("in", hk * 4 + g))
                    bk = g % 2
                    for kc in range(32):
                        mm(ps[:, bk, :], wt[:, kc, :], xnT[:, kc, 128:640], kc == 0, kc == 31, [wn] + xn_main, [("ps", bk)])
                    qknorm(ps[:, bk, :], 512, qkg[:, 0:1], QTs[hk % 2][:, g, :], kq, [("ps", bk)], [("QT", hk % 2, g)], sbank=2)

                for g in range(4):
                    q_chunk(0, g, kcnt)
                    kcnt += 1
                for hk in range(4):
                    QT = QTs[hk % 2]

                    def a_scores(n, hk=hk, QT=QT):
                        base = 3
                        for j in range(3):
                            kb = n + j
                            mm(ps[:, base + j, :], KT[:, hk, kb * 128:(kb + 1) * 128], QT[:, :, n * 128:(n + 1) * 128], True, True,
                               [("KT", hk, 0), ("KT", hk, 1)] + [("QT", hk % 2, g) for g in range(4)], [("ps", base + j)])

                    def a_softmax(n, hk=hk):
                        base = 3
                        i2 = n % 2
                        S2 = S2b[i2]
                        for g in range(4):
                            slope = 2.0 ** (-8.0 * (hk * 4 + g + 1) / 16.0)
                            P.op("dve", lambda e, S2=S2, g=g, slope=slope, base=base: e.scalar_tensor_tensor(
                                out=S2[:, :, g, :], in0=nd[:], scalar=slope, in1=ps[:, base:base + 3, g * 128:(g + 1) * 128], op0=ALU.mult, op1=ALU.add),
                                reads=[("ps", base), ("ps", base + 1), ("ps", base + 2)], writes=[("S2", i2, g)])
                        PT = PTb[i2]
                        s2r = [("S2", i2, g) for g in range(4)]
                        if tt == 0 and n == 0:
                            parts = [(0, 1, emk[:, 0:1]), (1, 3, None)]
                        elif tt == NT - 1 and n == 3:
                            parts = [(0, 2, None), (2, 3, emk[:, 1:2])]
                        else:
                            parts = [(0, 3, None)]
                        for pi, (j0, j1, bias) in enumerate(parts):
                            if bias is None:
                                P.op("act", lambda e, PT=PT, S2=S2, j0=j0, j1=j1: e.activation(out=PT[:, j0:j1], in_=S2[:, j0:j1], func=AF.Exp),
                                     reads=s2r, writes=[("PT", i2, pi)])
                            else:
                                P.op("act", lambda e, PT=PT, S2=S2, j0=j0, j1=j1, bias=bias: e.activation(out=PT[:, j0:j1], in_=S2[:, j0:j1], func=AF.Exp, bias=bias),
                                     reads=s2r, writes=[("PT", i2, pi)])
                        return [("PT", i2, pi) for pi in range(len(parts))]

                    def a_pv(n, ptr, hk=hk):
                        i2 = n % 2
                        PT = PTb[i2]
                        for j in range(3):
                            kb = n + j
                            mm(ps[:, 6, :], V[:, kb, hk * 128:(hk + 1) * 128], PT[:, j], j == 0, j == 2, ptr + [("V", kb, hk)], [("ps", 6)])
                        for j in range(3):
                            mm(ps[:, 7, :], onesb[:], PT[:, j], j == 0, j == 2, ptr, [("ps", 7)])
                        dn = denb[i2]
                        P.op("dve", lambda e, dn=dn: e.tensor_tensor(out=dn, in0=ps[:, 7, :].rearrange("p (g q) -> p g q", g=4),
                                                                  in1=esink[:, hk * 4:(hk + 1) * 4].unsqueeze(2).broadcast_to([128, 4, 128]), op=ALU.add),
                             reads=[("ps", 7)], writes=[("dn", i2)])
                        P.op("act", lambda e, dn=dn: e.activation(out=dn, in_=dn, func=AF.Ln), reads=[("dn", i2)], writes=[("dn", i2)])
                        P.op("act", lambda e, dn=dn: e.activation(out=dn, in_=dn, func=AF.Exp, scale=-1.0), reads=[("dn", i2)], writes=[("dn", i2)])
                        P.op("dve", lambda e, dn=dn, n=n: e.tensor_tensor(out=attnT[:, hk * 4:(hk + 1) * 4, n * 128:(n + 1) * 128],
                                                                       in0=ps[:, 6, :].rearrange("p (g q) -> p g q", g=4), in1=dn, op=ALU.mult),
                             reads=[("ps", 6), ("dn", i2)], writes=[("attnT", hk, n)])

                    for n in range(4):
                        a_scores(n)
                        ptr = a_softmax(n)
                        if hk < 3:
                            q_chunk(hk + 1, n, kcnt)
                            kcnt += 1
                        a_pv(n, ptr)
                attn_all = [("attnT", hk, n) for hk in range(4) for n in range(4)]
                if tt == 0:
                    dbg("attnT", attnT, [16, 512], BF16)
                if stop_after == "S1":
                    continue
                P.barrier()

                TB = 2176
                hbs = [cv(E + TB * i, [544], F32) for i in range(2)]
                ubs = [cv(E + TB * (2 + i), [544], F32) for i in range(2)]
                t1s = [cv(E + TB * (4 + i), [544], F32) for i in range(2)]
                t2s = [cv(E + TB * (6 + i), [544], F32) for i in range(2)]
                for i in range(16):
                    pb = 4 * (i % 2)
                    k2 = i % 2
                    hb, ub, t1, t2 = hbs[k2], ubs[k2], t1s[k2], t2s[k2]
                    wt, wn = W.get(P, ("in", 24 + i))
                    for kc in range(32):
                        mm(ps[:, pb, :], wt[:, kc, :], xnT[:, kc, 127:639], kc == 0, kc == 31, [wn] + xn_all, [("ps", pb)])
                    for kc in range(32):
                        mm(ps[:, pb + 1, 0:2], wt[:, kc, :], xnT[:, kc, 639:641], kc == 0, kc == 31, [wn] + xn_all, [("ps", pb + 1)])
                    P.op("act", lambda e, hb=hb, pb=pb: e.copy(out=hb[:, 0:512], in_=ps[:, pb, :]), reads=[("ps", pb)], writes=[("hb", k2, 0)])
                    P.op("act", lambda e, hb=hb, pb=pb: e.copy(out=hb[:, 512:514], in_=ps[:, pb + 1, 0:2]), reads=[("ps", pb + 1)], writes=[("hb", k2, 1)])
                    wt, wn = W.get(P, ("in", 56 + i))
                    for kc in range(32):
                        mm(ps[:, pb + 2, :], wt[:, kc, :], xnT[:, kc, 127:639], kc == 0, kc == 31, [wn] + xn_all, [("ps", pb + 2)])
                    for kc in range(32):
                        mm(ps[:, pb + 1, 2:4], wt[:, kc, :], xnT[:, kc, 639:641], kc == 0, kc == 31, [wn] + xn_all, [("ps", pb + 1)])
                    P.op("dve", lambda e, hb=hb, ub=ub, pb=pb: e.tensor_tensor(out=ub[:, 0:512], in0=ps[:, pb + 2, :], in1=hb[:, 0:512], op=ALU.mult),
                         reads=[("ps", pb + 2), ("hb", k2, 0)], writes=[("ub", k2, 0)])
                    P.op("dve", lambda e, hb=hb, ub=ub, pb=pb: e.tensor_tensor(out=ub[:, 512:514], in0=ps[:, pb + 1, 2:4], in1=hb[:, 512:514], op=ALU.mult),
                         reads=[("ps", pb + 1), ("hb", k2, 1)], writes=[("ub", k2, 1)])
                    wt, wn = W.get(P, ("in", 40 + i))
                    for kc in range(32):
                        mm(ps[:, pb + 3, :], wt[:, kc, :], xnT[:, kc, 128:640], kc == 0, kc == 31, [wn] + xn_main, [("ps", pb + 3)])
                    ubr = [("ub", k2, 0), ("ub", k2, 1)]
                    P.op("act", lambda e, ub=ub, t1=t1, i=i: e.mul(out=t1[:, 0:512], in_=ub[:, 0:512], mul=convw[:, 0, i:i + 1]), reads=ubr, writes=[("t1", k2)])
                    P.op("dve", lambda e, ub=ub, t1=t1, t2=t2, i=i: e.scalar_tensor_tensor(out=t2[:, 0:512], in0=ub[:, 1:513], scalar=convw[:, 1, i:i + 1], in1=t1[:, 0:512],
                                                                                      op0=ALU.mult, op1=ALU.add), reads=ubr + [("t1", k2)], writes=[("t2", k2)])
                    P.op("dve", lambda e, ub=ub, t1=t1, t2=t2, i=i: e.scalar_tensor_tensor(out=t1[:, 0:512], in0=ub[:, 2:514], scalar=convw[:, 2, i:i + 1], in1=t2[:, 0:512],
                                                                                      op0=ALU.mult, op1=ALU.add), reads=ubr + [("t2", k2)], writes=[("t1", k2)])
                    P.op("dve", lambda e, t1=t1, i=i, pb=pb: e.tensor_tensor(out=convT[:, i, :], in0=ps[:, pb + 3, :], in1=t1[:, 0:512], op=ALU.mult),
                         reads=[("ps", pb + 3), ("t1", k2)], writes=[("convT", i)])
                conv_all = [("convT", i) for i in range(16)]
                if tt == 0:
                    dbg("convT", convT, [16, 512], BF16)
                if stop_after == "S2":
                    continue

                S3o = E + 8 * TB
                sgas = [cv(S3o + 2048 * i, [512], F32) for i in range(2)]
                sgcs = [cv(S3o + 4096 + 2048 * i, [512], F32) for i in range(2)]
                tas = [cv(S3o + 8192 + 2048 * i, [512], F32) for i in range(2)]
                tbs = [cv(S3o + 12288 + 2048 * i, [512], F32) for i in range(2)]
                for c in range(32):
                    pb = 4 * (c % 2)
                    k2 = c % 2
                    sga, sgc, ta, tb = sgas[k2], sgcs[k2], tas[k2], tbs[k2]
                    wt, wn = W.get(P, ("in", 72 + c))
                    for kc in range(32):
                        mm(ps[:, pb, :], wt[:, kc, :], xnT[:, kc, 128:640], kc == 0, kc == 31, [wn] + xn_main, [("ps", pb)])
                    wt, wn = W.get(P, ("in", 104 + c))
                    for kc in range(32):
                        mm(ps[:, pb + 1, :], wt[:, kc, :], xnT[:, kc, 128:640], kc == 0, kc == 31, [wn] + xn_main, [("ps", pb + 1)])
                    wt, wn = W.get(P, ("oaoc", c))
                    for kc in range(16):
                        mm(ps[:, pb + 2, :], wt[:, kc, :], attnT[:, kc, :], kc == 0, kc == 15, [wn] + attn_all, [("ps", pb + 2)])
                    for kc in range(16):
                        mm(ps[:, pb + 3, :], wt[:, 16 + kc, :], convT[:, kc, :], kc == 0, kc == 15, [wn] + conv_all, [("ps", pb + 3)])
                    P.op("act", lambda e, sga=sga, pb=pb: e.activation(out=sga, in_=ps[:, pb, :], func=AF.Sigmoid), reads=[("ps", pb)], writes=[("sga", k2)])
                    P.op("act", lambda e, sgc=sgc, pb=pb: e.activation(out=sgc, in_=ps[:, pb + 1, :], func=AF.Sigmoid), reads=[("ps", pb + 1)], writes=[("sgc", k2)])
                    P.op("dve", lambda e, sga=sga, ta=ta, pb=pb: e.tensor_tensor(out=ta, in0=ps[:, pb + 2, :], in1=sga, op=ALU.mult),
                         reads=[("ps", pb + 2), ("sga", k2)], writes=[("ta", k2)])
                    P.op("dve", lambda e, sgc=sgc, tb=tb, pb=pb: e.tensor_tensor(out=tb, in0=ps[:, pb + 3, :], in1=sgc, op=ALU.mult),
                         reads=[("ps", pb + 3), ("sgc", k2)], writes=[("tb", k2)])
                    P.op("dve", lambda e, ta=ta, tb=tb, c=c: e.tensor_tensor(out=mT[:, c, :], in0=ta, in1=tb, op=ALU.add),
                         reads=[("ta", k2), ("tb", k2)], writes=[("mT", c)])
                if tt == 0:
                    dbg("mT", mT, [32, 512], BF16)
                if stop_after == "S3":
                    continue
                P.barrier()

                m_all = [("mT", c) for c in range(32)]
                xrs = [cv(E + 2048 * i, [4, 128], F32) for i in range(2)]
                xTs = [cv(E + 4096 + 2048 * i, [512], F32) for i in range(2)]
                for c in range(32):
                    k2 = c % 2
                    xr, xT = xrs[k2], xTs[k2]
                    src = xh[r_tile + 128:r_tile + 640, c * 128:(c + 1) * 128].rearrange("(b t) d -> t b d", t=128)
                    P.dma("sp", lambda e, xr=xr, src=src: e.dma_start(out=xr, in_=src), ("xr", k2), writes=[("xr", k2)])
                    for b in range(4):
                        P.op("pe", lambda e, xr=xr, b=b, k2=k2: e.transpose(out=ps[:, 2 + k2, b * 128:(b + 1) * 128], in_=xr[:, b, :], identity=identf[:]),
                             reads=[("xr", k2)], writes=[("ps", 2 + k2)])
                    P.op("act", lambda e, xT=xT, k2=k2: e.copy(out=xT, in_=ps[:, 2 + k2, :]), reads=[("ps", 2 + k2)], writes=[("xT", k2)])
                    wt, wn = W.get(P, ("out", c))
                    for kc in range(32):
                        mm(ps[:, k2, :], wt[:, kc, :], mT[:, kc, :], kc == 0, kc == 31, [wn] + m_all, [("ps", k2)])
                    P.op("dve", lambda e, xT=xT, k2=k2, c=c: e.tensor_tensor(out=acc[:, c, :], in0=ps[:, k2, :], in1=xT, op=ALU.add),
                         reads=[("ps", k2), ("xT", k2)], writes=[("acc", c)])
                acc_all = [("acc", c) for c in range(32)]
                if tt == 0:
                    dbg("acc1", acc, [32, 512], F32)
                if stop_after == "S4":
                    continue

                sq2s = [cv(E + 8192 + 2048 * i, [512], F32) for i in range(2)]
                rstd2 = cv(E + 12288, [512], F32)
                for c in range(32):
                    sq = sq2s[c % 2]
                    P.op("act", lambda e, sq=sq, c=c: e.activation(out=sq, in_=acc[:, c, :], func=AF.Square), reads=[("acc", c)], writes=[("sq2", c % 2)])
                    mm(ps[:, 4, :], onesf[:], sq, c == 0, c == 31, [("sq2", c % 2)], [("ps", 4)])
                P.op("act", lambda e: e.activation(out=rstd2, in_=ps[:, 4, :], func=AF.Sqrt, bias=epsc[:, 0:1], scale=1.0 / 4096),
                     reads=[("ps", 4)], writes=["rstd2"])
                P.op("dve", lambda e: e.reciprocal(out=rstd2, in_=rstd2), reads=["rstd2"], writes=["rstd2"])
                for c in range(32):
                    P.op("dve", lambda e, c=c: e.scalar_tensor_tensor(out=xn2T[:, c, :], in0=acc[:, c, :], scalar=g2[:, c:c + 1], in1=rstd2, op0=ALU.mult, op1=ALU.mult),
                         reads=[("acc", c), "rstd2"] + m_all, writes=[("xn2T", c)])
                xn2_all = [("xn2T", c) for c in range(32)]
                if tt == 0:
                    dbg("xn2T", xn2T, [32, 512], BF16)
                if stop_after == "S5":
                    continue

                WTs = [cv(E + 16384 * i, [16, 512], BF16) for i in range(2)]
                o6 = E + 16384
                qfs = [cv(o6 + 2048 * i, [512], F32) for i in range(2)]
                Ssbs = [cv(o6 + 4096 + 2048 * i, [4, 128], F32) for i in range(2)]
                wks = [cv(o6 + 8192 + 512 * i, [128], F32) for i in range(4)]
                vv = cv(o6 + 10240, [4, 16, 16], F32)
                idx = cv(o6 + 14336, [4, 16, 16], U32)
                cand = cv(o6 + 18432, [8, 16, 16], F32)
                cwks = [cv(o6 + 26624 + 1024 * i, [256], F32) for i in range(4)]
                tops = cv(o6 + 30720, [8, 16], F32)
                pos = cv(o6 + 31232, [8, 16], U32)
                sm = [cv(o6 + 31744 + 512 * i, [8, 16], F32) for i in range(10)]
                smu = [cv(o6 + 36864 + 512 * i, [8, 16], U32) for i in range(2)]
                es = cv(o6 + 37888, [8], F32)
                eq = cv(OFF_C + 8192, [8, 16, 16], F32)
                rT = cv(OFF_C, [3, 512], F32)
                for cq in range(16):
                    k2 = cq % 2
                    qf, Ssb = qfs[k2], Ssbs[k2]
                    wt, wn = W.get(P, ("qp", cq))
                    for kc in range(32):
                        mm(ps[:, k2, :], wt[:, kc, :], xn2T[:, kc, :], kc == 0, kc == 31, [wn] + xn2_all, [("ps", k2)])
                    P.op("act", lambda e, qf=qf, k2=k2: e.copy(out=qf, in_=ps[:, k2, :]), reads=[("ps", k2)], writes=[("qf", k2)])
                    for b in range(4):
                        mm(ps[:, 2 + k2, b * 128:(b + 1) * 128], qf[:, b * 128:(b + 1) * 128], subk[:, cq, :], True, True, [("qf", k2)], [("ps", 2 + k2)])
                    P.op("act", lambda e, Ssb=Ssb, k2=k2: e.copy(out=Ssb, in_=ps[:, 2 + k2, :].rearrange("p (b n) -> p b n", b=4)), reads=[("ps", 2 + k2)], writes=[("Ssb", k2)])
                    for b in range(4):
                        P.op("dve", lambda e, b=b, cq=cq, Ssb=Ssb: e.max(out=vv[:, b, cq, 0:8], in_=Ssb[:, b, :]), reads=[("Ssb", k2)], writes=[("v8a", b, cq)])
                    for b in range(4):
                        P.op("dve", lambda e, b=b, cq=cq, Ssb=Ssb: e.max_index(out=idx[:, b, cq, 0:8], in_max=vv[:, b, cq, 0:8], in_values=Ssb[:, b, :]),
                             reads=[("Ssb", k2), ("v8a", b, cq)], writes=[("i8a", b, cq)])
                    for b in range(4):
                        P.op("dve", lambda e, b=b, cq=cq, Ssb=Ssb: e.match_replace(out=wks[b], in_to_replace=vv[:, b, cq, 0:8], in_values=Ssb[:, b, :], imm_value=-1e30),
                             reads=[("Ssb", k2), ("v8a", b, cq)], writes=[("wk", b)])
                    for b in range(4):
                        P.op("dve", lambda e, b=b, cq=cq: e.max(out=vv[:, b, cq, 8:16], in_=wks[b]), reads=[("wk", b)], writes=[("v8b", b, cq)])
                    for b in range(4):
                        P.op("dve", lambda e, b=b, cq=cq: e.max_index(out=idx[:, b, cq, 8:16], in_max=vv[:, b, cq, 8:16], in_values=wks[b]),
                             reads=[("wk", b), ("v8b", b, cq)], writes=[("i8b", b, cq)])
                hsts = [cv(OFF_C + 6144 + 1024 * i, [512], BF16) for i in range(2)]

                def early_h0(il):
                    bk = 5 + il % 2
                    wt, wn = W.get(P, ("dn", il))
                    for kc in range(32):
                        mm(ps[:, bk, :], wt[:, kc, :], xn2T[:, kc, :], kc == 0, kc == 31, [wn] + xn2_all, [("ps", bk)])
                    P.op("act", lambda e, il=il, bk=bk: e.activation(out=WTs[0][:, il, :], in_=ps[:, bk, :], func=AF.Gelu), reads=[("ps", bk)], writes=[("WT", 0, il)])

                def early_h2(il):
                    bk = 5 + il % 2
                    wt, wn = W.get(P, ("dn", 32 + il))
                    for kc in range(32):
                        mm(ps[:, bk, :], wt[:, kc, :], xn2T[:, kc, :], kc == 0, kc == 31, [wn] + xn2_all, [("ps", bk)])
                    P.op("act", lambda e, il=il, bk=bk: e.activation(out=hsts[il % 2], in_=ps[:, bk, :], func=AF.Gelu), reads=[("ps", bk)], writes=[("hst", il % 2)])
                    P.dma("sp", lambda e, il=il, tt=tt: e.dma_start(out=hscr[tt, 0, :, il, :], in_=hsts[il % 2]), ("hs", il % 2), reads=[("hst", il % 2)])

                for il in range(8):
                    early_h0(il)
                for b in range(4):
                    vr = [("v8a", b, cq) for cq in range(16)] + [("v8b", b, cq) for cq in range(16)]
                    ir = [("i8a", b, cq) for cq in range(16)] + [("i8b", b, cq) for cq in range(16)]
                    vb4 = vv[:, b].rearrange("p (h two) a -> p h two a", two=2)
                    ib4 = idx[:, b].rearrange("p (h two) a -> p h two a", two=2)
                    v1, v2 = vb4[:, :, 0, :], vb4[:, :, 1, :]
                    P.op("dve", lambda e, v1=v1, v2=v2: e.tensor_tensor(out=cand, in0=v1.unsqueeze(3).broadcast_to([128, 8, 16, 16]),
                                                                     in1=v2.unsqueeze(2).broadcast_to([128, 8, 16, 16]), op=ALU.add), reads=vr, writes=["cand"])
                    for hg in range(2):
                        hs = list(range(hg * 4, hg * 4 + 4))
                        chs = {h: cand[:, h].rearrange("p a b -> p (a b)") for h in hs}
                        for h in hs:
                            P.op("dve", lambda e, h=h, ch=chs[h]: e.max(out=tops[:, h, 0:8], in_=ch), reads=["cand"], writes=[("tops", h, 0)])
                        for h in hs:
                            P.op("dve", lambda e, h=h, ch=chs[h]: e.max_index(out=pos[:, h, 0:8], in_max=tops[:, h, 0:8], in_values=ch), reads=["cand", ("tops", h, 0)], writes=[("pos", h, 0)])
                        for h in hs:
                            P.op("dve", lambda e, h=h, ch=chs[h]: e.match_replace(out=cwks[h % 4], in_to_replace=tops[:, h, 0:8], in_values=ch, imm_value=-1e30),
                                 reads=["cand", ("tops", h, 0)], writes=[("cw", h % 4)])
                        for h in hs:
                            P.op("dve", lambda e, h=h: e.max(out=tops[:, h, 8:16], in_=cwks[h % 4]), reads=[("cw", h % 4)], writes=[("tops", h, 1)])
                        for h in hs:
                            P.op("dve", lambda e, h=h: e.max_index(out=pos[:, h, 8:16], in_max=tops[:, h, 8:16], in_values=cwks[h % 4]), reads=[("cw", h % 4), ("tops", h, 1)], writes=[("pos", h, 1)])
                    tr = [("tops", h, k) for h in range(8) for k in range(2)]
                    pr = [("pos", h, k) for h in range(8) for k in range(2)]
                    dd, ee, gg, paf, pbf, i1f, i2f, ia, jb, rs_ = sm
                    pa, pb_ = smu
                    P.op("dve", lambda e: e.tensor_tensor(out=dd, in0=tops, in1=tops[:, :, 0:1].broadcast_to([128, 8, 16]), op=ALU.subtract), reads=tr, writes=["dd"])
                    P.op("act", lambda e: e.activation(out=ee, in_=dd, func=AF.Exp), reads=["dd"], writes=["ee"])
                    P.op("dve", lambda e: e.tensor_single_scalar(out=pa, in_=pos, scalar=4, op=ALU.logical_shift_right), reads=pr, writes=["pa"])
                    P.op("dve", lambda e: e.tensor_single_scalar(out=pb_, in_=pos, scalar=15, op=ALU.bitwise_and), reads=pr, writes=["pb"])
                    P.op("dve", lambda e, ib4=ib4: e.tensor_copy(out=i1f, in_=ib4[:, :, 0, :]), reads=ir, writes=["i1f"])
                    P.op("dve", lambda e, ib4=ib4: e.tensor_copy(out=i2f, in_=ib4[:, :, 1, :]), reads=ir, writes=["i2f"])
                    P.op("dve", lambda e: e.tensor_copy(out=paf, in_=pa), reads=["pa"], writes=["paf"])
                    P.op("dve", lambda e: e.tensor_copy(out=pbf, in_=pb_), reads=["pb"], writes=["pbf"])
                    P.op("dve", lambda e: e.tensor_reduce(out=es, in_=ee, axis=AX.X, op=ALU.add), reads=["ee"], writes=["es"])
                    P.op("dve", lambda e: e.reciprocal(out=es, in_=es), reads=["es"], writes=["es"])
                    P.op("dve", lambda e: e.tensor_tensor(out=gg, in0=ee, in1=es.unsqueeze(2).broadcast_to([128, 8, 16]), op=ALU.mult), reads=["ee", "es"], writes=["gg"])
                    iota16 = iota[:, 0:16].unsqueeze(1).unsqueeze(1).broadcast_to([128, 8, 16, 16])
                    for (pf, ixf, dst, nm) in ((paf, i1f, ia, "ia"), (pbf, i2f, jb, "jb")):
                        P.op("dve", lambda e, pf=pf: e.tensor_tensor(out=eq, in0=pf.unsqueeze(3).broadcast_to([128, 8, 16, 16]), in1=iota16, op=ALU.is_equal),
                             reads=["paf", "pbf"], writes=["eq"])
                        P.op("dve", lambda e, ixf=ixf: e.tensor_tensor(out=eq, in0=eq, in1=ixf.unsqueeze(2).broadcast_to([128, 8, 16, 16]), op=ALU.mult),
                             reads=["eq", "i1f", "i2f"], writes=["eq"])
                        P.op("dve", lambda e, dst=dst: e.tensor_reduce(out=dst, in_=eq, axis=AX.X, op=ALU.add), reads=["eq"], writes=[nm])
                    for k, (src_, nm) in enumerate(((ia, "ia"), (jb, "jb"), (gg, "gg"))):
                        P.op("pe", lambda e, src_=src_, k=k: e.transpose(out=ps[:, 4, k * 128:(k + 1) * 128], in_=src_.rearrange("p h k -> p (h k)"), identity=identf[:]),
                             reads=[nm], writes=[("ps", 4)])
                    P.op("act", lambda e, b=b: e.copy(out=rT[:, :, b * 128:(b + 1) * 128], in_=ps[:, 4, 0:384].rearrange("p (k t) -> p k t", k=3)),
                         reads=[("ps", 4)], writes=[("rT", b)])
                    for il in ((8, 9, 10, 11), (12, 13, 14, 15), (), ())[b]:
                        early_h0(il)
                    for il in ((0, 1), (2, 3, 4, 5), (6, 7, 8, 9, 10), (11, 12, 13, 14, 15))[b]:
                        early_h2(il)
                if tt == 0:
                    dbg("rT", rT, [3, 512], F32)
                if stop_after == "S6a":
                    continue
                P.barrier()

                L = cv(E + 32768, [32, 128], BF16)
                R = cv(E + 40960, [32, 128], BF16)
                Gsbs = [cv(E + 49152, [128, 32], BF16), cv(OFF_C + 8192, [128, 32], BF16)]
                iota_b = iota[:].unsqueeze(1).broadcast_to([128, 32, 128])
                for qb in range(16):
                    k2 = qb % 2
                    t0 = qb * 32
                    Gsb = Gsbs[k2]
                    jbv = rT[:, 1, t0:t0 + 32].unsqueeze(2).broadcast_to([128, 32, 128])
                    ggv = rT[:, 2, t0:t0 + 32].unsqueeze(2).broadcast_to([128, 32, 128])
                    iav = rT[:, 0, t0:t0 + 32].unsqueeze(2).broadcast_to([128, 32, 128])
                    P.op("dve", lambda e, jbv=jbv: e.tensor_tensor(out=L, in0=iota_b, in1=jbv, op=ALU.is_equal), writes=["L"])
                    P.op("dve", lambda e, ggv=ggv: e.tensor_tensor(out=L, in0=L, in1=ggv, op=ALU.mult), reads=["L"], writes=["L"])
                    P.op("dve", lambda e, iav=iav: e.tensor_tensor(out=R, in0=iota_b, in1=iav, op=ALU.is_equal), writes=["R"])
                    for t in range(32):
                        bk = 5 + (t // 4) % 2
                        mm(ps[:, bk, (t % 4) * 128:(t % 4 + 1) * 128], L[:, t, :], R[:, t, :], True, True, ["L", "R"], [("ps", bk)])
                        if t % 4 == 3:
                            src_ = ps[:, bk, :].rearrange("p (t i) -> p i t", t=4)
                            dst_ = Gsb[:, :, t - 3:t + 1]
                            P.op("act", lambda e, src_=src_, dst_=dst_: e.copy(out=dst_, in_=src_), reads=[("ps", bk)], writes=[("Gsb", k2, t // 4)])
                    P.dma("sp", lambda e, Gsb=Gsb, qb=qb, tt=tt: e.dma_start(out=gscr[tt, qb], in_=Gsb), ("gs", k2),
                          reads=[("Gsb", k2, u) for u in range(8)], writes=[("gscr", qb)])
                    bk = qb % 2
                    wt, wn = W.get(P, ("dn", 16 + qb))
                    for kc in range(32):
                        mm(ps[:, bk, :], wt[:, kc, :], xn2T[:, kc, :], kc == 0, kc == 31, [wn], [("ps", bk)])
                    P.op("act", lambda e, qb=qb, bk=bk: e.activation(out=WTs[1][:, qb, :], in_=ps[:, bk, :], func=AF.Gelu), reads=[("ps", bk)], writes=[("WT", 1, qb)])
                    bk = 2 + qb % 2
                    wt, wn = W.get(P, ("dn", 48 + qb))
                    for kc in range(32):
                        mm(ps[:, bk, :], wt[:, kc, :], xn2T[:, kc, :], kc == 0, kc == 31, [wn], [("ps", bk)])
                    P.op("act", lambda e, qb=qb, bk=bk: e.activation(out=hsts[qb % 2], in_=ps[:, bk, :], func=AF.Gelu), reads=[("ps", bk)], writes=[("hst", qb % 2)])
                    P.dma("sp", lambda e, qb=qb, tt=tt: e.dma_start(out=hscr[tt, 1, :, qb, :], in_=hsts[qb % 2]), ("hs", qb % 2), reads=[("hst", qb % 2)])
                if stop_after == "S6b":
                    continue
                P.barrier()

                Gcs = [cv(E + 32768 + 8192 * i, [16, 8, 32], BF16) for i in range(2)]
                gls = [cv(E + 49152 + 1024 * i, [512], BF16) for i in range(2)]
                for s in range(8):
                    WT = WTs[s % 2]
                    for gi in range(2):
                        i0 = s * 16 + gi * 8
                        src_ = gscr[tt, :, :, i0:i0 + 8, :].rearrange("q j i t -> j q i t")
                        P.dma("sp", lambda e, gi=gi, src_=src_: e.dma_start(out=Gcs[gi], in_=src_), ("gc", gi), writes=[("Gc", gi)])
                    if s in (2, 3):
                        P.dma("sp", lambda e, WT=WT, s=s, tt=tt: e.dma_start(out=WT, in_=hscr[tt, s - 2]), ("hl", s % 2),
                              writes=[("WT", s % 2, il) for il in range(16)])
                    for il in range(16):
                        i = s * 16 + il
                        k2 = il % 2
                        if s < 4:
                            P.op("dve", lambda e, WT=WT, il=il: e.tensor_tensor(out=WT[:, il, :].rearrange("p (q t) -> p q t", q=16), in0=WT[:, il, :].rearrange("p (q t) -> p q t", q=16),
                                                                            in1=Gcs[il // 8][:, :, il % 8, :], op=ALU.mult),
                                 reads=[("Gc", il // 8), ("WT", s % 2, il)], writes=[("WT", s % 2, il)])
                            continue
                        gl = gls[k2]
                        wt, wn = W.get(P, ("dn", i))
                        for kc in range(32):
                            mm(ps[:, k2, :], wt[:, kc, :], xn2T[:, kc, :], kc == 0, kc == 31, [wn], [("ps", k2)])
                        P.op("act", lambda e, gl=gl, k2=k2: e.activation(out=gl, in_=ps[:, k2, :], func=AF.Gelu), reads=[("ps", k2)], writes=[("gl", k2)])
                        P.op("dve", lambda e, gl=gl, WT=WT, il=il: e.tensor_tensor(out=WT[:, il, :].rearrange("p (q t) -> p q t", q=16), in0=gl.rearrange("p (q t) -> p q t", q=16),
                                                                                in1=Gcs[il // 8][:, :, il % 8, :], op=ALU.mult),
                             reads=[("gl", k2), ("Gc", il // 8)], writes=[("WT", s % 2, il)])
                    wtr = [("WT", s % 2, il) for il in range(16)]
                    for c in range(32):
                        bk = 2 + c % 2
                        if c % 2 == 0:
                            wt, wn = W.get(P, ("up", s, c // 2))
                        for il in range(16):
                            mm(ps[:, bk, :], wt[:, (c % 2) * 16 + il, :], WT[:, il, :], il == 0, il == 15, [wn] + wtr, [("ps", bk)])
                        P.op("dve", lambda e, c=c, bk=bk: e.tensor_tensor(out=acc[:, c, :], in0=ps[:, bk, :], in1=acc[:, c, :], op=ALU.add),
                             reads=[("ps", bk), ("acc", c)], writes=[("acc", c)])
                if tt == 0:
                    dbg("acc2", acc, [32, 512], F32)
                P.barrier()

                ots = [cv(OFF_B + 16384 * i, [4096], F32) for i in range(2)]
                ev = 0
                for b in range(4):
                    ot = ots[b % 2]
                    for g8 in range(8):
                        bk = g8 % 4
                        for q in range(4):
                            c = g8 * 4 + q
                            P.op("pe", lambda e, c=c, b=b, bk=bk, q=q: e.transpose(out=ps[:, bk, q * 128:(q + 1) * 128], in_=acc[:, c, b * 128:(b + 1) * 128], identity=identf[:]),
                                 reads=[("acc", c)], writes=[("ps", bk)])
                        if ev % 2 == 0:
                            P.op("act", lambda e, ot=ot, g8=g8, bk=bk: e.copy(out=ot[:, g8 * 512:(g8 + 1) * 512], in_=ps[:, bk, :]), reads=[("ps", bk)], writes=[("ot", b % 2, g8)])
                        else:
                            P.op("dve", lambda e, ot=ot, g8=g8, bk=bk: e.tensor_copy(out=ot[:, g8 * 512:(g8 + 1) * 512], in_=ps[:, bk, :]), reads=[("ps", bk)], writes=[("ot", b % 2, g8)])
                        ev += 1
                    r0 = tt * 512 + b * 128
                    P.dma("sp", lambda e, ot=ot, r0=r0: e.dma_start(out=out_d[r0:r0 + 128, :], in_=ot), ("ot", b % 2),
                          reads=[("ot", b % 2, g8) for g8 in range(8)])
                if tt == NT - 1:
                    P.barrier()

        class WStream:
            def __init__(self, keys=None):
                self.record = keys is None
                self.keys = [] if keys is None else keys
                self.pos = 0
                self.issued = 0

            def _issue(self, P, i):
                key = self.keys[i]
                src, kcn = wdram[key]
                slot = i % NB_W
                P.dma("pool", lambda e, src=src, kcn=kcn, slot=slot: e.dma_start(out=wbuf[slot][:, 0:kcn, :], in_=src), ("w", slot), writes=[("wb", slot)])

            def get(self, P, key):
                if self.record:
                    self.keys.append(key)
                    return wbuf[0], ("wb", 0)
                i = self.pos
                assert self.keys[i] == key, (self.keys[i], key)
                while self.issued < min(len(self.keys), i + NB_W):
                    self._issue(P, self.issued)
                    self.issued += 1
                self.pos += 1
                return wbuf[i % NB_W], ("wb", i % NB_W)

        Wr = WStream()
        body(Prog(nc), Wr)
        P = Prog(nc)
        body(P, WStream(Wr.keys))
        P.emit()
        info = {"nops": len(P.ops), "stats": P.stats, "nchunks": len(Wr.keys)}
    return nc, info, list(dbg_tensors.keys())


def _chunked(w, kcn):
    K, N = w.shape
    return np.ascontiguousarray(w.reshape(K // 128, 128, N // 128, 128).transpose(2, 1, 0, 3))


def prepare_inputs(x, norm1_g, w_in, q_norm_g, k_norm_g, sink_logits, conv_w, w_o_attn, w_o_conv,
                   w_out, norm2_g, w_q_peer, sub_keys, w_down, w_up):
    f = lambda a: np.asarray(a, dtype=np.float32)
    x = f(x)
    shared = {}
    shared["w_in_r"] = _chunked(f(w_in)[0], 32)
    shared["w_oaoc_r"] = np.ascontiguousarray(np.concatenate([_chunked(f(w_o_attn)[0], 16), _chunked(f(w_o_conv)[0], 16)], axis=2))
    shared["w_out_r"] = _chunked(f(w_out)[0], 32)
    shared["w_qp_r"] = _chunked(f(w_q_peer)[0], 32)
    wd = f(w_down)[0]
    shared["w_dn_r"] = np.ascontiguousarray(wd.reshape(128, 128, 32, 128).transpose(0, 3, 2, 1))
    wu = f(w_up)[0]
    shared["w_up_r"] = np.ascontiguousarray(wu.reshape(8, 16, 128, 16, 2, 128).transpose(0, 3, 2, 4, 1, 5)).reshape(8, 16, 128, 32, 128)
    shared["g1"] = np.ascontiguousarray(f(norm1_g)[0].reshape(32, 128).T)
    shared["g2"] = np.ascontiguousarray(f(norm2_g)[0].reshape(32, 128).T)
    shared["qkg"] = np.ascontiguousarray(np.stack([f(q_norm_g)[0], f(k_norm_g)[0]], axis=1))
    shared["sinkb"] = np.ascontiguousarray(np.broadcast_to(f(sink_logits)[0][None, :], (128, 16)))
    shared["convw"] = np.ascontiguousarray(f(conv_w)[0].reshape(3, 16, 128).transpose(2, 0, 1))
    sk = f(sub_keys)[0]
    shared["subk"] = np.ascontiguousarray(sk.reshape(16, 128, 128).transpose(2, 0, 1))
    shared["ident"] = np.eye(128, dtype=np.float32)
    s_ = np.arange(128)[:, None, None]
    j_ = np.arange(3)[None, :, None]
    q_ = np.arange(128)[None, None, :]
    dist = np.abs((j_ - 1) * 128 + s_ - q_)
    shared["nd"] = np.where(dist <= 128, -dist.astype(np.float32), np.float32(-1.0e7)).astype(np.float32)
    shared["iota"] = np.ascontiguousarray(np.broadcast_to(np.arange(128, dtype=np.float32)[None, :], (128, 128)))
    in_maps = []
    for c in range(NCORES):
        b, hf = c // 2, c % 2
        xhh = np.zeros((2304, 4096), np.float32)
        lo = hf * 2048 - 128
        hi = hf * 2048 + 2048 + 128
        slo, shi = max(lo, 0), min(hi, 4096)
        xhh[slo - lo:shi - lo] = x[b, slo:shi]
        em = np.zeros((128, 2), np.float32)
        if hf == 0:
            em[:, 0] = -30000.0
        else:
            em[:, 1] = -30000.0
        m = dict(shared)
        m["xh"] = xhh
        m["emask"] = em
        in_maps.append(m)
    return in_maps


_CACHE = {}


def kernel(**inputs):
    in_maps = prepare_inputs(**inputs)
    if "nc" not in _CACHE:
        _CACHE["nc"] = build_program(NT=4, debug=False)[0]
    nc = _CACHE["nc"]
    res = run_bass_kernel_spmd(nc, in_maps, core_ids=list(range(NCORES)))
    out = np.empty((4, 4096, 4096), np.float32)
    for c in range(NCORES):
        b, hf = c // 2, c % 2
        out[b, hf * 2048:(hf + 1) * 2048] = res.results[c]["out"]
    return out
```
